# Optimizing a Trainium2 kernel written in Bass

```python
import math
import jax, jax.numpy as jnp
from jax import lax
import numpy as np


D_MODEL = 1024
BATCH = 32
SEQ = 2048
DEPTH = 2

HEAD_DIM = D_MODEL // 16
MIX_WIDTH = D_MODEL // 2
FOX_HEADS = MIX_WIDTH // HEAD_DIM
SWA_Q_HEADS = MIX_WIDTH // HEAD_DIM
SWA_KV_HEADS = 2
SWA_WINDOW = 128
MLSTM_HEADS = 4
MLSTM_HEAD_DIM = MIX_WIDTH // MLSTM_HEADS
CONV_WIDTH = 4
D_FF = 4 * D_MODEL
N_BRANCHES = 3
BLOCK = 128
ROPE_THETA = 10000.0
EPS = 1e-6

IN_SPLITS = (
    MIX_WIDTH, MIX_WIDTH, MIX_WIDTH, FOX_HEADS,
    SWA_Q_HEADS * HEAD_DIM, SWA_KV_HEADS * HEAD_DIM, SWA_KV_HEADS * HEAD_DIM,
    MIX_WIDTH, MIX_WIDTH, MIX_WIDTH, MLSTM_HEADS, MLSTM_HEADS, MIX_WIDTH,
    N_BRANCHES * D_MODEL,
)
IN_WIDTH = sum(IN_SPLITS)

kernel_name = 'hybrid_fox_swa_mlstm_block'


def _rmsnorm(x, g):
    xf = x.astype(jnp.float32)
    y = xf * lax.rsqrt(jnp.mean(xf * xf, axis=-1, keepdims=True) + EPS)
    return (y * g.astype(jnp.float32)).astype(x.dtype)


def _rope_tables(seq, dim):
    inv = ROPE_THETA ** (-jnp.arange(0, dim, 2, dtype=jnp.float32) / dim)
    ang = jnp.arange(seq, dtype=jnp.float32)[:, None] * inv[None, :]
    return jnp.cos(ang), jnp.sin(ang)


def _rope(x, cos, sin):
    xf = x.astype(jnp.float32)
    x1, x2 = jnp.split(xf, 2, axis=-1)
    c = cos[None, :, None, :]
    s = sin[None, :, None, :]
    return jnp.concatenate([x1 * c - x2 * s, x1 * s + x2 * c], axis=-1).astype(x.dtype)


def _fox_attention(q, k, v, f_logit):
    B, S, H, Dh = q.shape
    nb = S // BLOCK
    scale = Dh ** -0.5
    c = jnp.cumsum(jax.nn.log_sigmoid(f_logit.astype(jnp.float32)), axis=1).transpose(0, 2, 1)
    qb = q.reshape(B, nb, BLOCK, H, Dh).transpose(1, 0, 2, 3, 4)
    cb = c.reshape(B, H, nb, BLOCK).transpose(2, 0, 1, 3)
    k_pos = jnp.arange(S)

    def one_block(args):
        i, qi, ci = args
        logits = jnp.einsum('bqhd,bshd->bhqs', qi, k, preferred_element_type=jnp.float32) * scale
        logits = logits + ci[..., :, None] - c[:, :, None, :]
        q_pos = i * BLOCK + jnp.arange(BLOCK)
        mask = k_pos[None, :] <= q_pos[:, None]
        p = jax.nn.softmax(jnp.where(mask, logits, -jnp.inf), axis=-1)
        return jnp.einsum('bhqs,bshd->bqhd', p.astype(v.dtype), v)

    out = lax.map(one_block, (jnp.arange(nb), qb, cb))
    return out.transpose(1, 0, 2, 3, 4).reshape(B, S, H * Dh)


def _swa_sink_attention(q, k, v, sinks):
    B, S, Hq, Dh = q.shape
    Hkv = k.shape[2]
    G = Hq // Hkv
    nb = S // BLOCK
    scale = Dh ** -0.5
    qb = q.reshape(B, nb, BLOCK, Hkv, G, Dh)

    def frame(t):
        tb = t.reshape(B, nb, BLOCK, Hkv, Dh)
        prev = jnp.pad(tb, ((0, 0), (1, 0), (0, 0), (0, 0), (0, 0)))[:, :-1]
        return jnp.concatenate([prev, tb], axis=2)

    kf, vf = frame(k), frame(v)
    logits = jnp.einsum('bnqhgd,bnkhd->bnhgqk', qb, kf, preferred_element_type=jnp.float32) * scale
    q_pos = jnp.arange(nb)[:, None, None] * BLOCK + jnp.arange(BLOCK)[None, :, None]
    k_pos = jnp.arange(nb)[:, None, None] * BLOCK - BLOCK + jnp.arange(2 * BLOCK)[None, None, :]
    mask = (k_pos <= q_pos) & (k_pos > q_pos - SWA_WINDOW) & (k_pos >= 0)
    logits = jnp.where(mask[None, :, None, None], logits, -jnp.inf)
    sink = jnp.broadcast_to(sinks.astype(jnp.float32).reshape(1, 1, Hkv, G, 1, 1), logits.shape[:-1] + (1,))
    p = jax.nn.softmax(jnp.concatenate([logits, sink], axis=-1), axis=-1)[..., :-1]
    out = jnp.einsum('bnhgqk,bnkhd->bnqhgd', p.astype(v.dtype), vf)
    return out.reshape(B, S, Hq * Dh)


def _causal_conv_silu(x, w, b):
    S = x.shape[1]
    W = w.shape[0]
    xp = jnp.pad(x, ((0, 0), (W - 1, 0), (0, 0)))
    y = b
    for j in range(W):
        y = y + xp[:, j:j + S] * w[j]
    return jax.nn.silu(y)


def _mlstm_chunkwise(q, k, v, i_logit, f_logit):
    B, S, H, Dh = q.shape
    L = BLOCK
    nc = S // L
    f32 = jnp.float32

    def chunks(t):
        return t.astype(f32).reshape(B, nc, L, H, -1).transpose(1, 0, 3, 2, 4)

    qc = chunks(q)
    kc = chunks(k) * (Dh ** -0.5)
    vc = chunks(v)
    ic = i_logit.astype(f32).reshape(B, nc, L, H).transpose(1, 0, 3, 2)
    bc = jnp.cumsum(jax.nn.log_sigmoid(f_logit.astype(f32)).reshape(B, nc, L, H).transpose(1, 0, 3, 2), axis=-1)
    causal = jnp.tril(jnp.ones((L, L), dtype=bool))

    def step(carry, xs):
        C, n, m = carry
        qi, ki, vi, ii, bi = xs
        a = bi + m[..., None]
        D = jnp.where(causal, bi[..., :, None] - bi[..., None, :] + ii[..., None, :], -jnp.inf)
        mt = jnp.maximum(a, jnp.max(D, axis=-1))
        w_inter = jnp.exp(a - mt)
        s = jnp.exp(D - mt[..., None]) * jnp.einsum('bhtd,bhsd->bhts', qi, ki)
        num = w_inter[..., None] * jnp.einsum('bhtd,bhde->bhte', qi, C) + jnp.einsum('bhts,bhse->bhte', s, vi)
        den = w_inter * jnp.einsum('bhtd,bhd->bht', qi, n) + jnp.sum(s, axis=-1)
        h = num / jnp.maximum(jnp.abs(den), jnp.exp(-mt))[..., None]
        bL = bi[..., -1]
        g = bL[..., None] - bi + ii
        m_new = jnp.maximum(bL + m, jnp.max(g, axis=-1))
        decay = jnp.exp(bL + m - m_new)
        wk = jnp.exp(g - m_new[..., None])
        C_new = decay[..., None, None] * C + jnp.einsum('bhs,bhsd,bhse->bhde', wk, ki, vi)
        n_new = decay[..., None] * n + jnp.einsum('bhs,bhsd->bhd', wk, ki)
        return (C_new, n_new, m_new), h

    init = (jnp.zeros((B, H, Dh, Dh), f32), jnp.zeros((B, H, Dh), f32), jnp.zeros((B, H), f32))
    _, hs = lax.scan(step, init, (qc, kc, vc, ic, bc))
    return hs.transpose(1, 0, 3, 2, 4).reshape(B, S, H, Dh)


def _hybrid_layer(x, cos, sin, norm_mix, w_in, fox_f_bias, fox_q_norm, fox_k_norm,
                  swa_q_norm, swa_k_norm, swa_sinks, conv_w, conv_b, mlstm_i_bias,
                  mlstm_f_bias, mlstm_out_norm, w_branch, w_out, norm_mlp, w_up, w_down):
    B, S, _ = x.shape
    h = _rmsnorm(x, norm_mix)
    u = jnp.einsum('bsd,de->bse', h, w_in)
    offsets = np.cumsum(IN_SPLITS)[:-1].tolist()
    fq, fk, fv, ff, sq, sk, sv, mq, mk, mv, mi, mf, mo, gl = jnp.split(u, offsets, axis=-1)

    def heads(t, n_heads):
        return t.reshape(B, S, n_heads, -1)

    y_fox = _fox_attention(_rmsnorm(heads(fq, FOX_HEADS), fox_q_norm),
                           _rmsnorm(heads(fk, FOX_HEADS), fox_k_norm),
                           heads(fv, FOX_HEADS), ff + fox_f_bias)
    sq = _rope(_rmsnorm(heads(sq, SWA_Q_HEADS), swa_q_norm), cos, sin)
    sk = _rope(_rmsnorm(heads(sk, SWA_KV_HEADS), swa_k_norm), cos, sin)
    y_swa = _swa_sink_attention(sq, sk, heads(sv, SWA_KV_HEADS), swa_sinks)
    qk = _causal_conv_silu(jnp.concatenate([mq, mk], axis=-1), conv_w, conv_b)
    mq, mk = jnp.split(qk, 2, axis=-1)
    hm = _mlstm_chunkwise(heads(mq, MLSTM_HEADS), heads(mk, MLSTM_HEADS), heads(mv, MLSTM_HEADS),
                          mi + mlstm_i_bias, mf + mlstm_f_bias)
    hm = _rmsnorm(hm, mlstm_out_norm.reshape(MLSTM_HEADS, MLSTM_HEAD_DIM)).reshape(B, S, MIX_WIDTH)
    y_mlstm = (hm * jax.nn.sigmoid(mo.astype(jnp.float32))).astype(x.dtype)

    gates = jax.nn.sigmoid(gl.reshape(B, S, N_BRANCHES, D_MODEL))
    merged = (gates[:, :, 0] * jnp.einsum('bsc,cd->bsd', y_fox, w_branch[0])
              + gates[:, :, 1] * jnp.einsum('bsc,cd->bsd', y_swa, w_branch[1])
              + gates[:, :, 2] * jnp.einsum('bsc,cd->bsd', y_mlstm, w_branch[2]))
    x = x + jnp.einsum('bsd,de->bse', merged, w_out)

    h2 = _rmsnorm(x, norm_mlp)
    act = jnp.square(jax.nn.relu(jnp.einsum('bsd,df->bsf', h2, w_up)))
    return x + jnp.einsum('bsf,fd->bsd', act, w_down)


def setup_inputs(seed: int = 0) -> dict:
    key = jax.random.key(seed)
    ks = jax.random.split(key, 20)
    f32 = jnp.float32
    nrm = lambda k, shape, s: jax.random.normal(k, shape, f32) * s
    gain = lambda k, shape: 1.0 + 0.05 * jax.random.normal(k, shape, f32)
    return {
        'x': nrm(ks[0], (BATCH, SEQ, D_MODEL), 1.0),
        'norm_mix': gain(ks[1], (DEPTH, D_MODEL)),
        'w_in': nrm(ks[2], (DEPTH, D_MODEL, IN_WIDTH), D_MODEL ** -0.5),
        'fox_f_bias': 3.0 + 0.5 * jax.random.normal(ks[3], (DEPTH, FOX_HEADS), f32),
        'fox_q_norm': gain(ks[4], (DEPTH, HEAD_DIM)),
        'fox_k_norm': gain(ks[5], (DEPTH, HEAD_DIM)),
        'swa_q_norm': gain(ks[6], (DEPTH, HEAD_DIM)),
        'swa_k_norm': gain(ks[7], (DEPTH, HEAD_DIM)),
        'swa_sinks': nrm(ks[8], (DEPTH, SWA_Q_HEADS), 0.5),
        'conv_w': nrm(ks[9], (DEPTH, CONV_WIDTH, 2 * MIX_WIDTH), CONV_WIDTH ** -0.5),
        'conv_b': nrm(ks[10], (DEPTH, 2 * MIX_WIDTH), 0.02),
        'mlstm_i_bias': nrm(ks[11], (DEPTH, MLSTM_HEADS), 0.1),
        'mlstm_f_bias': 3.0 + 0.5 * jax.random.normal(ks[12], (DEPTH, MLSTM_HEADS), f32),
        'mlstm_out_norm': gain(ks[13], (DEPTH, MIX_WIDTH)),
        'w_branch': nrm(ks[14], (DEPTH, N_BRANCHES, MIX_WIDTH, D_MODEL), MIX_WIDTH ** -0.5),
        'w_out': nrm(ks[15], (DEPTH, D_MODEL, D_MODEL), D_MODEL ** -0.5),
        'norm_mlp': gain(ks[16], (DEPTH, D_MODEL)),
        'w_up': nrm(ks[17], (DEPTH, D_MODEL, D_FF), D_MODEL ** -0.5),
        'w_down': nrm(ks[18], (DEPTH, D_FF, D_MODEL), D_FF ** -0.5),
    }


def reference(x, norm_mix, w_in, fox_f_bias, fox_q_norm, fox_k_norm, swa_q_norm, swa_k_norm,
              swa_sinks, conv_w, conv_b, mlstm_i_bias, mlstm_f_bias, mlstm_out_norm, w_branch,
              w_out, norm_mlp, w_up, w_down):
    S = x.shape[1]
    cos, sin = _rope_tables(S, HEAD_DIM)
    for l in range(DEPTH):
        x = _hybrid_layer(x, cos, sin, norm_mix[l], w_in[l], fox_f_bias[l], fox_q_norm[l],
                          fox_k_norm[l], swa_q_norm[l], swa_k_norm[l], swa_sinks[l], conv_w[l],
                          conv_b[l], mlstm_i_bias[l], mlstm_f_bias[l], mlstm_out_norm[l],
                          w_branch[l], w_out[l], norm_mlp[l], w_up[l], w_down[l])
    return x
```

```python
import math
import numpy as np
import ml_dtypes
import concourse.bass as bass
import concourse.mybir as mybir
from concourse.bass_utils import run_bass_kernel_spmd

F32 = mybir.dt.float32
BF16 = mybir.dt.bfloat16
AF = mybir.ActivationFunctionType
ALU = mybir.AluOpType

D = 1024
T = 2048
KC = 8
NG = 4
NB = 16
DEPTH = 2
NCORES = 8
IN_W = 7440
NAUX = 784
EPS = 1e-6
O_FQ, O_FK, O_FV, O_FF = 0, 512, 1024, 1536
O_SQ, O_SK, O_SV = 1544, 2056, 2184
O_MQ, O_MK, O_MV, O_MI, O_MF, O_MO, O_GL = 2312, 2824, 3336, 3848, 3852, 3856, 4368
PP_GMIX, PP_GMLP = 0, 8
PP_FQN, PP_FKN, PP_SQN, PP_SKN, PP_SQNS, PP_SKNS = 16, 17, 18, 19, 20, 21
PP_CONVW, PP_CONVB = 22, 54
PP_MON, PP_MFB, PP_SINK = 62, 66, 70
NPP = 78
NDMASEM = 12
COMPUTE = ("pe", "dve", "act", "pool")


class Buf:
    __slots__ = ("lw", "rd")

    def __init__(self):
        self.lw = None
        self.rd = []


def bufs(*shape):
    if len(shape) == 1:
        return [Buf() for _ in range(shape[0])]
    return [bufs(*shape[1:]) for _ in range(shape[0])]


class BufCache:
    def __init__(self):
        self.objs = []
        self.i = 0

    def rewind(self):
        self.i = 0

    def buf(self):
        if self.i == len(self.objs):
            self.objs.append(Buf())
        b = self.objs[self.i]
        self.i += 1
        return b

    def bufs(self, *shape):
        if len(shape) == 1:
            return [self.buf() for _ in range(shape[0])]
        return [self.bufs(*shape[1:]) for _ in range(shape[0])]


class Op:
    __slots__ = ("eng", "fn", "deps", "sig", "sem", "val", "is_dma", "prev_val")

    def __init__(self, eng, fn, is_dma):
        self.eng = eng
        self.fn = fn
        self.deps = []
        self.sig = is_dma
        self.sem = None
        self.val = 0
        self.prev_val = 0
        self.is_dma = is_dma


class Prog:
    def __init__(self, nc):
        self.nc = nc
        self.ops = {e: [] for e in ("pe", "dve", "act", "pool", "sp")}
        self.pending = {e: [] for e in self.ops}

    def barrier(self):
        last = [lst[-1] for e, lst in self.ops.items() if e in COMPUTE and lst]
        for e in self.pending:
            if e != "pe":
                self.pending[e] = list(last)

    def add(self, eng, fn, reads=(), writes=(), dma=False, skip_barrier=False):
        op = Op(eng, fn, dma)
        deps = {}
        for b in reads:
            if b.lw is not None:
                deps[id(b.lw)] = (b.lw, True)
        for b in writes:
            if b.lw is not None and id(b.lw) not in deps:
                deps[id(b.lw)] = (b.lw, False)
            for r in b.rd:
                if id(r) not in deps:
                    deps[id(r)] = (r, False)
        if not skip_barrier:
            for d in self.pending[eng]:
                if id(d) not in deps:
                    deps[id(d)] = (d, True)
            self.pending[eng] = []
        for d, raw in deps.values():
            if d is op:
                continue
            if (not d.is_dma) and (not dma) and d.eng == eng:
                if eng == "pe":
                    continue
            op.deps.append(d)
            d.sig = True
        for b in reads:
            b.rd.append(op)
        for b in writes:
            b.lw = op
            b.rd = []
        self.ops[eng].append(op)
        return op

    def pe(self, fn, reads=(), writes=()):
        return self.add("pe", fn, reads, writes)

    def dve(self, fn, reads=(), writes=()):
        return self.add("dve", fn, reads, writes)

    def act(self, fn, reads=(), writes=()):
        return self.add("act", fn, reads, writes)

    def pool(self, fn, reads=(), writes=()):
        return self.add("pool", fn, reads, writes)

    def dma(self, fn, reads=(), writes=(), q="sp", skip_barrier=False):
        return self.add(q, fn, reads, writes, dma=True, skip_barrier=skip_barrier)

    def emit(self, final_ops=()):
        nc = self.nc
        esem = {e: nc.alloc_semaphore("s_" + e) for e in COMPUTE}
        for e, lst in self.ops.items():
            c = 0
            for op in lst:
                if op.is_dma:
                    continue
                if op.sig:
                    c += 1
                    op.sem = esem[e]
                    op.val = c
        for e, lst in self.ops.items():
            if not any(op.is_dma for op in lst):
                continue
            nds = 40 if e == "pool" else NDMASEM
            dsem = [nc.alloc_semaphore("s_dma_%s%d" % (e, i)) for i in range(nds)]
            dcnt = [0] * nds
            di = 0
            for op in lst:
                if op.is_dma:
                    k = di % nds
                    di += 1
                    op.prev_val = dcnt[k]
                    dcnt[k] += 16
                    op.sem = dsem[k]
                    op.val = dcnt[k]
        final_ops = list(final_ops)

        def run(ename, eng):
            known = {}
            for op in self.ops[ename]:
                need = {}
                for d in op.deps:
                    key = id(d.sem)
                    if key not in need or need[key][1] < d.val:
                        need[key] = (d.sem, d.val)
                if op.is_dma and op.prev_val > 0:
                    key = id(op.sem)
                    if key not in need or need[key][1] < op.prev_val:
                        need[key] = (op.sem, op.prev_val)
                for key, (sem, val) in need.items():
                    if known.get(key, 0) >= val:
                        continue
                    eng.wait_ge(sem, val)
                    known[key] = val
                ins = op.fn(eng)
                if op.is_dma:
                    ins.then_inc(op.sem, 16)
                elif op.sig:
                    ins.then_inc(op.sem, 1)
            if ename == "sp":
                for op in final_ops:
                    key = id(op.sem)
                    if known.get(key, 0) < op.val:
                        eng.wait_ge(op.sem, op.val)
                        known[key] = op.val

        with nc.Block() as block:
            @block.sync
            def _(e):
                run("sp", e)

            @block.tensor
            def _(e):
                run("pe", e)

            @block.vector
            def _(e):
                run("dve", e)

            @block.scalar
            def _(e):
                run("act", e)

            @block.gpsimd
            def _(e):
                run("pool", e)


class Ring:
    def __init__(self, items):
        self.items = items
        self.i = 0

    def next(self):
        it = self.items[self.i % len(self.items)]
        self.i += 1
        return it


class Scratch:
    def __init__(self, ap2d):
        self.ap = ap2d
        self.n = ap2d.shape[1]
        self.off = 0

    def reset(self):
        self.off = 0

    def take(self, shape, dtype):
        n = int(np.prod(shape))
        nb = n * (2 if dtype == F32 else 1)
        nb = (nb + 15) // 16 * 16
        assert self.off + nb <= self.n, ("scratch overflow", self.off, nb, self.n)
        v = self.ap[:, self.off:self.off + n * (2 if dtype == F32 else 1)]
        self.off += nb
        if dtype == F32:
            v = v.bitcast(F32)
        if len(shape) == 2:
            v = v.rearrange("p (a b) -> p a b", a=shape[0])
        elif len(shape) == 3:
            v = v.rearrange("p (a b c) -> p a b c", a=shape[0], b=shape[1])
        return v


def build(n_seq, depth, dbg=False, phases="MSFGP"):
    nc = bass.Bass("TRN2", target_bir_lowering=False)
    P = Prog(nc)
    dten = nc.dram_tensor
    xT_d = dten("xT", [n_seq, D, T], F32, kind="ExternalInput").ap()
    w_in_d = dten("w_in", [depth, D, IN_W], F32, kind="ExternalInput").ap()
    w_aux_d = dten("w_aux", [depth, D, NAUX], F32, kind="ExternalInput").ap()
    w_br_d = dten("w_br", [depth, 1536, D], F32, kind="ExternalInput").ap()
    w_out_d = dten("w_out", [depth, D, D], F32, kind="ExternalInput").ap()
    w_up_d = dten("w_up", [depth, D, 4 * D], F32, kind="ExternalInput").ap()
    w_dn_d = dten("w_dn", [depth, 4 * D, D], F32, kind="ExternalInput").ap()
    pp_d = dten("pp", [128, depth * NPP], F32, kind="ExternalInput").ap()
    brow_d = dten("brow", [128, depth * 16], F32, kind="ExternalInput").ap()
    rope_d = dten("rope", [2, 128, T], F32, kind="ExternalInput").ap()
    cmat_d = dten("cmat", [128, 11 * 128], BF16, kind="ExternalInput").ap()
    cf32_d = dten("cf32", [128, 260], F32, kind="ExternalInput").ap()
    yT_d = dten("yT", [n_seq, D, T], F32, kind="ExternalOutput").ap()
    if dbg:
        dbg_y = dten("dbg_y", [12, 128, T], BF16, kind="ExternalOutput").ap()
        dbg_x1 = dten("dbg_x1", [8, 128, T], F32, kind="ExternalOutput").ap()
    wsc = {}
    wsc_buf = {}
    specs = {"w_in": (w_in_d, D, IN_W), "w_aux": (w_aux_d, D, NAUX), "w_br": (w_br_d, 1536, D),
             "w_out": (w_out_d, D, D), "w_up": (w_up_d, D, 4 * D), "w_dn": (w_dn_d, 4 * D, D)}
    PR = 256
    for name, (src, rows, cols) in specs.items():
        for l in range(depth):
            wsc[(name, l)] = dten("sc_%s_%d" % (name, l), [rows, cols], BF16).ap()
            wsc_buf[(name, l)] = bufs(rows // 128)

    lazy_casts = []

    def flush_casts(key=None):
        if key is not None and not any(k == key for k, _ in lazy_casts):
            return
        while lazy_casts:
            k, fn_ = lazy_casts.pop(0)
            fn_()
            if key is not None and not any(k2 == key for k2, _ in lazy_casts):
                return

    def cast_tick(every=[0]):
        every[0] += 1
        if lazy_casts and every[0] % 2 == 0:
            lazy_casts.pop(0)[1]()

    def cast_weights(l, lazy=False, names=("w_aux", "w_in", "w_br", "w_out", "w_up", "w_dn")):
        for name in names:
            src, rows, cols = specs[name]
            t = wsc[(name, l)]
            bl = wsc_buf[(name, l)]
            pr = 128 if lazy else PR
            for r in range(rows // pr):
                def issue(t=t, src=src, r=r, pr=pr, bl=bl):
                    P.dma(lambda e: e.dma_start(out=t[r * pr:(r + 1) * pr, :], in_=src[l, r * pr:(r + 1) * pr, :]),
                          writes=bl[r * (pr // 128):(r + 1) * (pr // 128)], q="pool")
                if lazy:
                    lazy_casts.append(((name, l), issue))
                else:
                    issue()

    cast_weights(0, names=("w_aux", "w_in"))
    cast_weights(0, lazy=True, names=("w_br", "w_out", "w_up", "w_dn"))
    for l_ in range(1, depth):
        cast_weights(l_, lazy=True)

    sb = nc.alloc_sbuf_tensor
    xT = sb("xT_sb", [128, KC, T], F32)
    hT = sb("hT_sb", [128, KC, T], BF16)
    yT = sb("yT_sb", [128, 12, T], BF16)
    NW = 8
    wt = [sb("wt%d" % i, [128, 1024], BF16) for i in range(NW)]
    wt_buf = bufs(NW)
    wring = Ring(list(range(NW)))
    scr_t = sb("scr", [128, 18432], BF16)
    cmat = sb("cmat_sb", [128, 11 * 128], BF16)
    cf32 = sb("cf32_sb", [128, 260], F32)
    pp = sb("pp_sb", [128, depth * NPP], F32)
    brow = sb("brow_sb", [128, depth * 16], F32)
    wsm = sb("wsm_sb", [128, KC, 16], BF16)
    gsm = sb("gsm_sb", [128, 2, 256], F32)
    wexp = sb("wexp_sb", [128, 16, 4], F32)
    esink = sb("esink_sb", [128, 8], F32)
    nmfb = sb("nmfb_sb", [128, 4], F32)
    ident = cmat[:, 0:128]
    maskUL = cmat[:, 128:384]
    maskU = cmat[:, 128:256]
    maskL = cmat[:, 256:384]
    bd64 = cmat[:, 384:512]
    on128 = cmat[:, 512:640]
    rperm = cmat[:, 640:768]
    on1024 = cmat[:, 768:896]
    sel2 = cmat[:, 896:1024]
    negUL = cmat[:, 1024:1280]
    negU = cmat[:, 1024:1152]
    negL = cmat[:, 1152:1280]
    triU32 = cf32[:, 0:128]
    ones32 = cf32[:, 128:256]
    b_const = Buf()
    b_pp = Buf()
    P.dma(lambda e: e.dma_start(out=cmat[:], in_=cmat_d), writes=[b_const])
    P.dma(lambda e: e.dma_start(out=cf32[:], in_=cf32_d), writes=[b_const])
    P.dma(lambda e: e.dma_start(out=pp[:], in_=pp_d), writes=[b_pp])
    P.dma(lambda e: e.dma_start(out=brow[:], in_=brow_d), writes=[b_pp])

    psum = [nc.alloc_psum_tensor("ps%d" % i, [128, 512], F32) for i in range(8)]
    ps_buf = bufs(8)
    RR = Ring([0, 1, 2, 3])
    AR = Ring([4, 5, 6, 7])

    xbuf = bufs(NG)
    hbuf = bufs(NG)
    ybuf = bufs(12, NG)
    b_scr = Buf()
    S = Scratch(scr_t[:, :])

    def gsl(n):
        return slice(n * 512, (n + 1) * 512)

    def load_w(name, l, c0, ncols, kc0=0, nkc=KC, dst_off=0, tile=None):
        if tile is None:
            tile = wring.next()
        flush_casts((name, l))
        cast_tick()
        src = wsc[(name, l)][kc0 * 128:(kc0 + nkc) * 128, c0:c0 + ncols].rearrange("(kc p) c -> p kc c", p=128)
        dst = wt[tile][:, dst_off:dst_off + nkc * ncols].rearrange("p (kc c) -> p kc c", kc=nkc)
        P.dma(lambda e: e.dma_start(out=dst, in_=src), reads=wsc_buf[(name, l)][kc0:kc0 + nkc], writes=[wt_buf[tile]], skip_barrier=True)
        return tile, dst

    def mm(out, lhsT, rhs, start, stop, reads, writes, skip=False):
        if skip:
            P.pe(lambda e: e.matmul(out, lhsT=lhsT, rhs=rhs, start=start, stop=stop, skip_group_check=True), reads=reads, writes=writes)
        else:
            P.pe(lambda e: e.matmul(out, lhsT=lhsT, rhs=rhs, start=start, stop=stop), reads=reads, writes=writes)

    def proj_fm(wv, tile, n, bank, ncol=128, c0=0):
        cast_tick()
        for kc in range(KC):
            mm(psum[bank][0:ncol, :], wv[:, kc, c0:c0 + ncol], hT[:, kc, gsl(n)], kc == 0, kc == KC - 1,
               [wt_buf[tile], hbuf[n]], [ps_buf[bank]])

    def rstd_from(bank_stats, rstd_ap, wbuf, half=False):
        P.act(lambda e: e.activation(out=rstd_ap, in_=psum[bank_stats][:, :], func=AF.Ln, bias=EPS),
              reads=[ps_buf[bank_stats]], writes=[wbuf])
        if half:
            P.act(lambda e: e.activation(out=rstd_ap, in_=rstd_ap, func=AF.Exp, scale=-0.5, bias=math.log(0.5)), reads=[wbuf], writes=[wbuf])
        else:
            P.act(lambda e: e.activation(out=rstd_ap, in_=rstd_ap, func=AF.Exp, scale=-0.5), reads=[wbuf], writes=[wbuf])

    def rmsnorm_x(l, gcol0):
        S.reset()
        sq = [S.take([512], BF16) for _ in range(4)]
        sqb = bufs(4)
        rs = [S.take([512], F32) for _ in range(2)]
        rsb = bufs(2)
        for n in range(NG):
            bank = RR.next()
            for kc in range(KC):
                i = kc % 4
                fn = (lambda e, i=i, kc=kc, n=n: e.activation(out=sq[i], in_=xT[:, kc, gsl(n)], func=AF.Square))
                if kc % 2 == 0:
                    P.act(fn, reads=[xbuf[n]], writes=[sqb[i]])
                else:
                    P.pool(lambda e, i=i, kc=kc, n=n: e.tensor_tensor(out=sq[i], in0=xT[:, kc, gsl(n)], in1=xT[:, kc, gsl(n)], op=ALU.mult),
                           reads=[xbuf[n]], writes=[sqb[i]])
                mm(psum[bank][:, :], on1024, sq[i], kc == 0, kc == KC - 1, [b_const, sqb[i]], [ps_buf[bank]])
            r = n % 2
            rstd_from(bank, rs[r], rsb[r])
            for kc in range(KC):
                P.dve(lambda e, kc=kc, n=n, r=r: e.scalar_tensor_tensor(
                    out=hT[:, kc, gsl(n)], in0=xT[:, kc, gsl(n)], scalar=pp[:, l * NPP + gcol0 + kc:l * NPP + gcol0 + kc + 1],
                    in1=rs[r], op0=ALU.mult, op1=ALU.mult), reads=[xbuf[n], rsb[r], b_pp], writes=[hbuf[n]])
        P.barrier()

    def gates_phase(l):
        S.reset()
        Gs = S.take([256], F32)
        sp = S.take([256], F32)
        totS = S.take([256], F32)
        incl = S.take([256], F32)
        offsp = gsm[:, 0, :]
        csp = gsm[:, 1, :]
        v3 = lambda a: a.rearrange("p (b c) -> p b c", b=16)
        flush_casts(("w_aux", l))
        P.dma(lambda e: e.dma_start(out=wsm[:], in_=wsc[("w_aux", l)][:, 768:784].rearrange("(kc p) c -> p kc c", p=128)),
              reads=wsc_buf[("w_aux", l)], writes=[b_scr])
        bank = RR.next()
        for b in range(NB):
            for kc in range(KC):
                mm(psum[bank][:, b * 16:(b + 1) * 16], hT[:, kc, b * 128:(b + 1) * 128], wsm[:, kc, :], kc == 0, kc == KC - 1,
                   [b_scr, hbuf[b // 4]], [ps_buf[bank]])
        P.dve(lambda e: e.tensor_tensor(out=v3(Gs), in0=v3(psum[bank][:, 0:256]),
                                        in1=brow[:, l * 16:(l + 1) * 16].unsqueeze(1).to_broadcast([128, 16, 16]), op=ALU.add),
              reads=[ps_buf[bank], b_pp], writes=[b_scr])
        P.act(lambda e: e.activation(out=sp, in_=Gs, func=AF.Exp, scale=-1.0), reads=[b_scr], writes=[b_scr])
        P.act(lambda e: e.activation(out=sp, in_=sp, func=AF.Ln, bias=1.0), reads=[b_scr], writes=[b_scr])
        bank2 = RR.next()
        mm(psum[bank2][:, 0:256], triU32, sp, True, True, [b_scr, b_const], [ps_buf[bank2]])
        mm(psum[bank2][:, 256:512], ones32, sp, True, True, [b_scr, b_const], [ps_buf[bank2]])
        P.dve(lambda e: e.tensor_copy(out=sp, in_=psum[bank2][:, 0:256]), reads=[ps_buf[bank2]], writes=[b_scr])
        P.dve(lambda e: e.tensor_copy(out=totS, in_=psum[bank2][:, 256:512]), reads=[ps_buf[bank2]], writes=[b_scr])
        for h in range(8):
            P.dve(lambda e, h=h: e.tensor_tensor_scan(out=v3(incl)[:, :, h], data0=ones32[:, 0:16], data1=v3(totS)[:, :, h],
                                                      initial=0.0, op0=ALU.mult, op1=ALU.add), reads=[b_scr, b_const], writes=[b_scr])
        P.dve(lambda e: e.tensor_tensor(out=offsp, in0=incl, in1=totS, op=ALU.subtract), reads=[b_scr], writes=[b_scr])
        P.dve(lambda e: e.tensor_tensor(out=csp, in0=sp, in1=offsp, op=ALU.add), reads=[b_scr], writes=[b_scr])
        P.dve(lambda e: e.tensor_tensor(out=wexp[:], in0=v3(Gs)[:, :, 8:12], in1=v3(sp)[:, :, 12:16], op=ALU.add), reads=[b_scr], writes=[b_scr])
        P.act(lambda e: e.activation(out=wexp[:], in_=wexp[:], func=AF.Exp, bias=-0.5 * math.log(128.0) + math.log(0.5)), reads=[b_scr], writes=[b_scr])
        P.act(lambda e: e.activation(out=esink[:], in_=pp[:, l * NPP + PP_SINK:l * NPP + PP_SINK + 8], func=AF.Exp), reads=[b_pp], writes=[b_scr])
        P.dve(lambda e: e.tensor_scalar(out=nmfb[:], in0=pp[:, l * NPP + PP_MFB:l * NPP + PP_MFB + 4], scalar1=0.5, scalar2=None, op0=ALU.mult),
              reads=[b_pp], writes=[b_scr])
        P.barrier()
        return csp, offsp

    def mlstm_phase(l):
        ppc = lambda c: pp[:, l * NPP + c:l * NPP + c + 1]
        BC = BufCache()

        def head(h):
            BC.rewind()
            S.reset()
            b_sgt_holder = [BC.buf()]
            XS = Scratch(yT[:, 0:8, :].rearrange("p a t -> p (a t)"))
            pre = XS.take([2064], F32)
            ebrep = XS.take([T], F32)
            qT = XS.take([T], BF16)
            kT = XS.take([T], BF16)
            Vp = XS.take([NB, 132], BF16)
            so = S.take([T], BF16)
            kTok = S.take([NB, 128], BF16)
            wrep = S.take([NB, 128], BF16)
            b_pre, b_eb, b_so, b_q, b_k, b_ktok, b_vp, b_wrep = BC.bufs(8)
            RG = S.take([6 * 512], F32)
            spn = [RG[:, i * 512:(i + 1) * 512] for i in (0, 1)]
            csn = [RG[:, i * 512:(i + 1) * 512] for i in (2, 3)]
            acc = [RG[:, i * 512:(i + 1) * 512] for i in (4, 5)]
            b_spn, b_csn, b_acc = BC.bufs(2), BC.bufs(2), BC.bufs(2)
            A3 = RG[:, 0:2064]
            b_A3 = [b_spn[0], b_spn[1], b_csn[0], b_csn[1], b_acc[0]]
            mS = [S.take([128], BF16) for _ in range(3)]
            b_mS = BC.bufs(3)
            Cb_all = S.take([NB, 128], BF16)
            b_Cb = BC.buf()
            a16 = S.take([16], F32)
            b_a16 = BC.buf()
            nrep_all = kTok
            d1 = [S.take([512], F32), XS.take([512], F32)]
            sgt = d1[0]
            b_sgt = b_sgt_holder[0]
            rsn = [S.take([512], F32), S.take([512], F32)]
            sqn = [S.take([512], BF16), XS.take([512], BF16)]
            hg = d1
            b_d1, b_sqn, b_rsn = BC.bufs(2), BC.bufs(2), BC.bufs(2)
            b_d1[0] = b_sgt_holder[0]
            b_hg = b_d1
            tf, wf = load_w("w_aux", l, 256 + h * 128, 128)
            to_, wo_ = load_w("w_in", l, O_MO + h * 128, 128)
            tq_, wq_ = load_w("w_in", l, O_MQ + h * 128, 128)
            tk_, wk_ = load_w("w_in", l, O_MK + h * 128, 128)
            tv_, wv_ = load_w("w_in", l, O_MV + h * 128, 128)
            win = [[pre[:, (2 * w + r) * 515:(2 * w + r + 1) * 515] for r in range(2)] for w in range(2)]
            b_win = BC.bufs(2, 2)
            b_allwin = [b_win[0][0], b_win[0][1], b_win[1][0], b_win[1][1]]
            P.dve(lambda e: e.tensor_copy(out=Vp[:, :, 128:129], in_=wexp[:, :, h:h + 1]), reads=[b_scr], writes=[b_vp])
            P.pool(lambda e: e.tensor_copy(out=wrep, in_=wexp[:, :, h:h + 1].to_broadcast([128, NB, 128])), reads=[b_scr], writes=[b_wrep])

            def transposes(g):
                bank = AR.next()
                pb = psum[bank][:, :].bitcast(BF16)
                for b_ in range(4):
                    blk = g * 4 + b_
                    P.pe(lambda e, pb=pb, b_=b_, blk=blk: e.transpose(pb[:, b_ * 128:(b_ + 1) * 128], kT[:, blk * 128:(blk + 1) * 128], ident),
                         reads=[b_k, b_const], writes=[ps_buf[bank]])
                P.act(lambda e, pb=pb, g=g: e.activation(out=kTok[:, g * 4:(g + 1) * 4, :], in_=pb[:, 0:512].rearrange("p (a b) -> p a b", a=4), func=AF.Identity),
                      reads=[ps_buf[bank]], writes=[b_ktok])

            for n in range(NG):
                r = n % 2
                bF = RR.next()
                proj_fm(wf, tf, n, bF)
                P.act(lambda e, r=r, bF=bF: e.activation(out=spn[r], in_=psum[bF][:, :], func=AF.Tanh, scale=0.5, bias=nmfb[:, h:h + 1]),
                      reads=[ps_buf[bF], b_scr], writes=[b_spn[r]])
                bO_ = RR.next()
                proj_fm(wo_, to_, n, bO_)
                P.act(lambda e, n=n, bO_=bO_: e.activation(out=so[:, gsl(n)], in_=psum[bO_][:, :], func=AF.Tanh, scale=0.5), reads=[ps_buf[bO_]], writes=[b_so])
                for w, (wv2, tt2) in enumerate(((wq_, tq_), (wk_, tk_))):
                    bQ = RR.next()
                    proj_fm(wv2, tt2, n, bQ)
                    P.act(lambda e, w=w, r=r, bQ=bQ: e.activation(out=win[w][r][:, 3:515], in_=psum[bQ][:, :], func=AF.Identity),
                          reads=[ps_buf[bQ]], writes=[b_win[w][r]])
                    if n == 0:
                        P.pool(lambda e, w=w, r=r: e.memset(win[w][r][:, 0:3], 0.0), writes=[b_win[w][r]])
                    else:
                        P.pool(lambda e, w=w, r=r: e.tensor_copy(out=win[w][r][:, 0:3], in_=win[w][1 - r][:, 512:515]),
                               reads=[b_win[w][1 - r]], writes=[b_win[w][r]])
                bank = AR.next()
                for b_ in range(4):
                    blk = n * 4 + b_
                    for kc in range(KC):
                        mm(psum[bank][:, b_ * 128:(b_ + 1) * 128], hT[:, kc, blk * 128:(blk + 1) * 128], wv_[:, kc, :], kc == 0, kc == KC - 1,
                           [wt_buf[tv_], hbuf[n]], [ps_buf[bank]])
                for b_ in range(4):
                    blk = n * 4 + b_
                    P.dve(lambda e, b_=b_, blk=blk, bank=bank: e.tensor_scalar(out=Vp[:, blk, 0:128], in0=psum[bank][:, b_ * 128:(b_ + 1) * 128],
                                                                               scalar1=wexp[:, blk, h:h + 1], scalar2=None, op0=ALU.mult),
                          reads=[ps_buf[bank], b_scr], writes=[b_vp])
                if n > 0:
                    transposes(n - 1)
                P.dve(lambda e, r=r: e.tensor_scalar(out=spn[r], in0=spn[r], scalar1=1.0, scalar2=0.5, op0=ALU.add, op1=ALU.mult),
                      reads=[b_spn[r]], writes=[b_spn[r]])
                for j in range(4):
                    P.dve(lambda e, r=r, j=j, n=n: e.tensor_tensor_scan(out=ebrep[:, n * 512 + j * 128:n * 512 + (j + 1) * 128], data0=spn[r][:, j * 128:(j + 1) * 128],
                                                                        data1=ones32, initial=1.0, op0=ALU.mult, op1=ALU.mult),
                          reads=[b_spn[r], b_const], writes=[b_eb])
                for w, (dst, b_dst) in enumerate(((qT, b_q), (kT, b_k))):
                    cc = w * 4 + h
                    cw = lambda j, cc=cc: ppc(PP_CONVW + cc * 4 + j)
                    P.act(lambda e, w=w, r=r, cw=cw, cc=cc: e.activation(out=acc[w], in_=win[w][r][:, 3:515], func=AF.Identity,
                                                                        scale=cw(3), bias=ppc(PP_CONVB + cc)),
                          reads=[b_win[w][r], b_pp], writes=[b_acc[w]])
                    for j in (2, 1, 0):
                        P.dve(lambda e, w=w, r=r, j=j, cw=cw: e.scalar_tensor_tensor(out=acc[w], in0=win[w][r][:, j:j + 512], scalar=cw(j),
                                                                                    in1=acc[w], op0=ALU.mult, op1=ALU.add),
                              reads=[b_win[w][r], b_acc[w], b_pp], writes=[b_acc[w]])
                    P.act(lambda e, w=w: e.activation(out=sgt, in_=acc[w], func=AF.Tanh, scale=0.5), reads=[b_acc[w]], writes=[b_sgt])
                    P.dve(lambda e, n=n, w=w, dst=dst: e.scalar_tensor_tensor(out=dst[:, gsl(n)], in0=sgt, scalar=1.0, in1=acc[w], op0=ALU.add, op1=ALU.mult),
                          reads=[b_acc[w], b_sgt], writes=[b_dst])
            transposes(NG - 1)
            b_pre_all = [b_pre] + b_allwin
            tU3 = pre.rearrange("p (e c) -> p e c", c=16)
            ebv = ebrep.rearrange("p (c t) -> p c t", t=128)
            P.dve(lambda e: e.tensor_copy(out=a16, in_=ebv[:, :, 127]), reads=[b_eb], writes=[b_a16])
            P.dve(lambda e: e.memset(a16[:, 0:1], 0.0), writes=[b_a16])
            for c in range(NB - 1):
                bU = RR.next()
                mm(psum[bU][:, 0:129], kTok[:, c, :], Vp[:, c, 0:129], True, True, [b_ktok, b_vp], [ps_buf[bU]])
                P.dve(lambda e, bU=bU, c=c: e.tensor_scalar(out=tU3[:, :, c], in0=psum[bU][:, 0:129], scalar1=ebrep[:, c * 128 + 127:c * 128 + 128],
                                                            scalar2=None, op0=ALU.mult), reads=[ps_buf[bU], b_eb], writes=b_pre_all)
            P.dve(lambda e: e.memset(tU3[:, :, 15:16], 0.0), writes=b_pre_all)
            P.act(lambda e: e.activation(out=A3.rearrange("p (e c) -> p e c", c=16), in_=a16.unsqueeze(1).to_broadcast([128, 129, 16]), func=AF.Identity),
                  reads=[b_a16], writes=b_A3)
            zeros128 = cmat[:, 1280:1408]
            gb = {}

            def emit_s(c):
                csl = slice(c * 128, (c + 1) * 128)
                bS = RR.next()
                mm(psum[bS][:, 0:128], kT[:, csl], qT[:, csl], True, True, [b_k, b_q], [ps_buf[bS]])
                mi = c % 3
                P.dve(lambda e, mi=mi, bS=bS: e.tensor_tensor(out=mS[mi], in0=psum[bS][:, 0:128], in1=maskU, op=ALU.mult),
                      reads=[ps_buf[bS], b_const], writes=[b_mS[mi]])

            def intra(g):
                bN1, bN2 = AR.next(), AR.next()
                gb[g] = (bN1, bN2)
                for bk in (bN1, bN2):
                    mm(psum[bk][:, :], zeros128, qT[:, 0:512], True, False, [b_const, b_q], [ps_buf[bk]], skip=True)
                emit_s(g * 4)
                for j in range(4):
                    c = g * 4 + j
                    if j < 3:
                        emit_s(c + 1)
                    jsl = slice(j * 128, (j + 1) * 128)
                    mi = c % 3
                    mm(psum[bN1][:, jsl], Vp[:, c, 0:128], mS[mi], False, False, [b_vp, b_mS[mi]], [ps_buf[bN1]], skip=True)
                    mm(psum[bN2][:, jsl], wrep[:, c, :], mS[mi], False, False, [b_wrep, b_mS[mi]], [ps_buf[bN2]], skip=True)

            def inter(g):
                bN1, bN2 = gb[g]
                for j in range(4):
                    c = g * 4 + j
                    if c == 0:
                        continue
                    csl = slice(c * 128, (c + 1) * 128)
                    jsl = slice(j * 128, (j + 1) * 128)
                    mm(psum[bN1][:, jsl], Cb_all[:, c, :], qT[:, csl], False, False, [b_Cb, b_q], [ps_buf[bN1]], skip=True)
                    mm(psum[bN2][:, jsl], nrep_all[:, c, :], qT[:, csl], False, False, [b_ktok, b_q], [ps_buf[bN2]], skip=True)

            def E1(g):
                bN1, bN2 = gb[g]
                r = g % 2
                P.dve(lambda e: e.scalar_tensor_tensor(out=d1[r], in0=psum[bN2][:, :], scalar=0.5, in1=ebrep[:, gsl(g)], op0=ALU.mult, op1=ALU.mult),
                      reads=[ps_buf[bN2], b_eb], writes=[b_d1[r]])
                P.dve(lambda e: e.scalar_tensor_tensor(out=d1[r], in0=d1[r], scalar=-1.0, in1=d1[r], op0=ALU.mult, op1=ALU.max),
                      reads=[b_d1[r]], writes=[b_d1[r]])
                P.dve(lambda e: e.tensor_scalar(out=d1[r], in0=d1[r], scalar1=1.0, scalar2=None, op0=ALU.max), reads=[b_d1[r]], writes=[b_d1[r]])
                P.act(lambda e: e.activation(out=d1[r], in_=d1[r], func=AF.Ln), reads=[b_d1[r]], writes=[b_d1[r]])
                P.act(lambda e: e.activation(out=d1[r], in_=d1[r], func=AF.Exp, scale=-1.0), reads=[b_d1[r]], writes=[b_d1[r]])

            def E2(g):
                bN1, bN2 = gb[g]
                r = g % 2
                P.dve(lambda e: e.scalar_tensor_tensor(out=d1[r], in0=d1[r], scalar=0.5, in1=ebrep[:, gsl(g)], op0=ALU.mult, op1=ALU.mult),
                      reads=[b_d1[r], b_eb], writes=[b_d1[r]])
                P.dve(lambda e: e.tensor_tensor(out=hg[r], in0=psum[bN1][:, :], in1=d1[r], op=ALU.mult),
                      reads=[ps_buf[bN1], b_d1[r]], writes=[b_hg[r]])
                P.act(lambda e: e.activation(out=sqn[r], in_=hg[r], func=AF.Square), reads=[b_hg[r]], writes=[b_sqn[r]])
                bst = RR.next()
                mm(psum[bst][:, :], on128, sqn[r], True, True, [b_const, b_sqn[r]], [ps_buf[bst]])
                rstd_from(bst, rsn[r], b_rsn[r], half=True)

            def E3(g):
                r = g % 2
                P.dve(lambda e: e.scalar_tensor_tensor(out=hg[r], in0=hg[r], scalar=ppc(PP_MON + h), in1=rsn[r], op0=ALU.mult, op1=ALU.mult),
                      reads=[b_hg[r], b_rsn[r], b_pp], writes=[b_hg[r]])
                P.dve(lambda e: e.scalar_tensor_tensor(out=yT[:, 8 + h, gsl(g)], in0=so[:, gsl(g)], scalar=1.0, in1=hg[r], op0=ALU.add, op1=ALU.mult),
                      reads=[b_hg[r], b_so], writes=[ybuf[8 + h][g]])

            intra(0)
            intra(1)
            P.dve(lambda e: e.tensor_tensor_scan(out=pre[:, 0:2064], data0=A3, data1=pre[:, 0:2064],
                                                 initial=0.0, op0=ALU.mult, op1=ALU.add), reads=[b_pre] + b_A3, writes=[b_pre])
            P.act(lambda e: e.activation(out=Cb_all[:, 1:16, :], in_=tU3[:, 0:128, 0:15].rearrange("p e c -> p c e"), func=AF.Identity),
                  reads=[b_pre], writes=[b_Cb])
            P.dve(lambda e: e.tensor_copy(out=nrep_all[:, 1:16, :], in_=tU3[:, 128, 0:15].unsqueeze(2).to_broadcast([128, 15, 128])),
                  reads=[b_pre], writes=[b_ktok])
            for step in (lambda: inter(0), lambda: E1(0), lambda: inter(1), lambda: E1(1), lambda: E2(0), lambda: intra(2), lambda: E3(0),
                         lambda: E2(1), lambda: inter(2), lambda: E1(2), lambda: E3(1), lambda: intra(3), lambda: E2(2), lambda: inter(3),
                         lambda: E1(3), lambda: E3(2), lambda: E2(3), lambda: E3(3)):
                step()

        for h in range(4):
            head(h)
        P.barrier()

    def qk_norm_fm(bankA, sq_t, b_sq, rs_t, b_rs, sel):
        P.act(lambda e: e.activation(out=sq_t, in_=psum[bankA][:, :], func=AF.Square), reads=[ps_buf[bankA]], writes=[b_sq])
        bst = RR.next()
        mm(psum[bst][:, :], sel, sq_t, True, True, [b_const, b_sq], [ps_buf[bst]])
        rstd_from(bst, rs_t, b_rs)

    def swa_phase(l):
        ppc = lambda c: pp[:, l * NPP + c:l * NPP + c + 1]

        BC = BufCache()

        def group(g):
            BC.rewind()
            S.reset()
            XS = Scratch(yT[:, 0:4, :].rearrange("p a t -> p (a t)"))
            qTs = XS.take([2, T], BF16)
            kTs0 = XS.take([T], BF16)
            Va = XS.take([NB, 128], BF16)
            kTm = [kTs0, S.take([T], BF16)]
            b_qs = BC.bufs(2, NG)
            b_ks = BC.bufs(NG)
            b_va = BC.buf()
            rope_sb = S.take([2, T], F32)
            b_rope = BC.buf()
            P.dma(lambda e: e.dma_start(out=rope_sb[:, 0, :], in_=rope_d[0]), writes=[b_rope])
            P.dma(lambda e: e.dma_start(out=rope_sb[:, 1, :], in_=rope_d[1]), writes=[b_rope])
            sq_t = [S.take([512], BF16) for _ in range(2)]
            rs_t = [S.take([512], F32)] * 2
            qn = [S.take([512], BF16) for _ in range(3)]
            t1 = S.take([512], F32)
            t2 = S.take([512], F32)
            b_sq, b_rs, b_qn = BC.bufs(2), [BC.buf()] * 2, BC.bufs(3)
            b_t1, b_t2 = BC.buf(), BC.buf()
            NPT = 4
            Pt = [S.take([256], BF16) for _ in range(NPT)]
            b_Pt = BC.bufs(NPT)
            rc = S.take([512], F32)
            b_rc = BC.buf()
            tq0, wq0 = load_w("w_in", l, O_SQ + g * 256, 128)
            tq1, wq1 = load_w("w_in", l, O_SQ + g * 256 + 128, 128)
            tk, wk = load_w("w_aux", l, g * 128, 128)
            tv, wvv = load_w("w_in", l, O_SV + g * 64, 64)
            P.pool(lambda e: e.memset(Va[:, :, 64:128], 1.0), writes=[b_va])
            P.pool(lambda e: e.memset(kTm[0][64:128, :], 0.0), writes=b_ks)
            P.pool(lambda e: e.memset(kTm[1][0:64, :], 0.0), writes=b_ks)
            for gg in range(4):
                bank = RR.next()
                for b in range(4):
                    blk = gg * 4 + b
                    for kc in range(KC):
                        mm(psum[bank][:, b * 64:(b + 1) * 64], hT[:, kc, blk * 128:(blk + 1) * 128], wvv[:, kc, :], kc == 0, kc == KC - 1,
                           [wt_buf[tv], hbuf[gg]], [ps_buf[bank]])
                P.act(lambda e, gg=gg, bank=bank: e.activation(out=Va[:, gg * 4:(gg + 1) * 4, 0:64], in_=psum[bank][:, 0:256].rearrange("p (a b) -> p a b", a=4),
                                                               func=AF.Identity), reads=[ps_buf[bank]], writes=[b_va])
            items = [(n, w) for n in range(NG) for w in range(3)]
            st = {}

            def stA(k):
                n, w = items[k]
                bank = AR.next()
                st[("A", k)] = bank
                if w < 2:
                    proj_fm((wq0, wq1)[w], (tq0, tq1)[w], n, bank)
                else:
                    proj_fm(wk, tk, n, bank)

            def stB(k):
                bank = st[("A", k)]
                r = k % 2
                P.act(lambda e, r=r, bank=bank: e.activation(out=sq_t[r], in_=psum[bank][:, :], func=AF.Square), reads=[ps_buf[bank]], writes=[b_sq[r]])
                bst = RR.next()
                st[("B", k)] = bst
                mm(psum[bst][:, :], bd64, sq_t[r], True, True, [b_const, b_sq[r]], [ps_buf[bst]])

            def stC(k):
                bank, bst = st[("A", k)], st[("B", k)]
                r = k % 2
                q3 = k % 3
                rstd_from(bst, rs_t[r], b_rs[r])
                P.dve(lambda e, r=r, q3=q3, bank=bank: e.tensor_tensor(out=qn[q3], in0=psum[bank][:, :], in1=rs_t[r], op=ALU.mult),
                      reads=[ps_buf[bank], b_rs[r]], writes=[b_qn[q3]])
                bC = RR.next()
                st[("C", k)] = bC
                mm(psum[bC][:, :], rperm, qn[q3], True, True, [b_const, b_qn[q3]], [ps_buf[bC]])

            def stD(k):
                n, w = items[k]
                bC = st[("C", k)]
                q3 = k % 3
                gc, gs = (PP_SQN, PP_SQNS) if w < 2 else (PP_SKN, PP_SKNS)
                if w < 2:
                    dsts, bd = [(slice(0, 128), qTs[:, w, gsl(n)])], b_qs[w][n]
                else:
                    dsts, bd = [(slice(0, 64), kTm[0][0:64, gsl(n)]), (slice(64, 128), kTm[1][64:128, gsl(n)])], b_ks[n]
                P.dve(lambda e, q3=q3, gc=gc, n=n: e.scalar_tensor_tensor(out=t1, in0=qn[q3], scalar=ppc(gc), in1=rope_sb[:, 0, gsl(n)], op0=ALU.mult, op1=ALU.mult),
                      reads=[b_qn[q3], b_rope, b_pp], writes=[b_t1])
                P.dve(lambda e, gs=gs, n=n, bC=bC: e.scalar_tensor_tensor(out=t2, in0=psum[bC][:, :], scalar=ppc(gs), in1=rope_sb[:, 1, gsl(n)], op0=ALU.mult, op1=ALU.mult),
                      reads=[ps_buf[bC], b_rope, b_pp], writes=[b_t2])
                for rws, dst in dsts:
                    P.pool(lambda e, dst=dst, rws=rws: e.tensor_tensor(out=dst, in0=t1[rws, :], in1=t2[rws, :], op=ALU.add), reads=[b_t1, b_t2], writes=[bd])

            stages = (stA, stB, stC, stD)
            for step in range(len(items) + len(stages) - 1):
                for si, f in enumerate(stages):
                    k = step - si
                    if 0 <= k < len(items):
                        f(k)
            its = []
            for qc in range(2):
                for hh in range(2):
                    for I in range(NG):
                        jl = list(range(max(4 * I - 1, 0), 4 * I + 4))
                        for j in jl:
                            its.append((qc, hh, I, j, j == jl[0], j == jl[-1]))
            LA = 2

            def emit_s(k):
                qc, hh, I, j, first, last = its[k]
                rows = slice(hh * 64, hh * 64 + 64)
                i_lo, i_hi = max(j, 4 * I), min(j + 1, 4 * I + 3)
                ncol = (i_hi - i_lo + 1) * 128
                bS = RR.next()
                mm(psum[bS][:, 0:ncol], kTm[hh][:, j * 128:(j + 1) * 128], qTs[:, qc, i_lo * 128:(i_hi + 1) * 128], True, False,
                   [b_ks[j // 4], b_qs[qc][I]], [ps_buf[bS]])
                msk = negUL[:, 0:ncol] if i_lo == j else negL
                mm(psum[bS][:, 0:ncol], ident, msk, False, True, [b_const], [ps_buf[bS]])
                p = k % NPT
                P.act(lambda e, p=p, bS=bS, ncol=ncol: e.activation(out=Pt[p][:, 0:ncol], in_=psum[bS][:, 0:ncol], func=AF.Exp, scale=0.125),
                      reads=[ps_buf[bS]], writes=[b_Pt[p]])

            def emit_pv(k):
                qc, hh, I, j, first, last = its[k]
                rows = slice(hh * 64, hh * 64 + 64)
                head = g * 4 + qc * 2 + hh
                ych = 4 + g * 2 + qc
                i_lo, i_hi = max(j, 4 * I), min(j + 1, 4 * I + 3)
                ncol = (i_hi - i_lo + 1) * 128
                col0 = (i_lo - 4 * I) * 128
                if first:
                    st["bO"] = AR.next()
                bO = st["bO"]
                p = k % NPT
                mm(psum[bO][:, col0:col0 + ncol], Va[:, j, :], Pt[p][:, 0:ncol], first, last, [b_va, b_Pt[p]], [ps_buf[bO]], skip=True)
                if last:
                    P.act(lambda e, bO=bO, head=head: e.activation(out=rc[0:64, :], in_=psum[bO][64:128, :], func=AF.Ln, bias=esink[64:128, head:head + 1]),
                          reads=[ps_buf[bO], b_scr], writes=[b_rc])
                    P.act(lambda e: e.activation(out=rc[0:64, :], in_=rc[0:64, :], func=AF.Exp, scale=-1.0), reads=[b_rc], writes=[b_rc])
                    P.dve(lambda e, bO=bO, rows=rows, ych=ych, I=I: e.tensor_tensor(out=yT[rows, ych, gsl(I)], in0=psum[bO][0:64, :], in1=rc[0:64, :], op=ALU.mult),
                          reads=[ps_buf[bO], b_rc], writes=[ybuf[ych][I]])

            for k in range(len(its) + LA):
                if k < len(its):
                    emit_s(k)
                if k >= LA:
                    emit_pv(k - LA)

        for g in range(2):
            group(g)
        P.barrier()

    def fox_phase(l, csp, offsp):
        ppc = lambda c: pp[:, l * NPP + c:l * NPP + c + 1]
        v3 = lambda a: a.rearrange("p (b c) -> p b c", b=16)
        v4 = lambda a: a.rearrange("p (I i c) -> p I i c", I=4, i=4)

        BC = BufCache()

        def pair(c):
            BC.rewind()
            S.reset()
            qTm = [S.take([T], BF16) for _ in range(2)]
            kTm = [S.take([T], BF16) for _ in range(2)]
            Va = S.take([NB, 2, 128], BF16)
            bias = S.take([2, 16, 4], F32)
            dsh = S.take([2, 16], F32)
            rsh = S.take([2, 16], F32)
            hi16 = S.take([2, 16], BF16)
            v1 = S.take([2, 16], F32)
            valb = S.take([2, 16], BF16)
            b_q, b_k = BC.bufs(NG), BC.bufs(NG)
            b_va, b_bias, b_sh = BC.buf(), BC.buf(), BC.buf()
            sq_t = [S.take([512], BF16) for _ in range(2)]
            rs_t = [S.take([512], F32)] * 2
            b_sq, b_rs = BC.bufs(2), [BC.buf()] * 2
            NPT = 4
            Pt = [S.take([512], BF16) for _ in range(NPT)]
            b_Pt = BC.bufs(NPT)
            rc = [S.take([512], F32)] * 2
            b_rc = [BC.buf()] * 2
            tq, wq = load_w("w_in", l, O_FQ + c * 128, 128)
            tk, wk = load_w("w_in", l, O_FK + c * 128, 128)
            tv, wvv = load_w("w_in", l, O_FV + c * 128, 128)
            for hh in range(2):
                h = 2 * c + hh
                P.dve(lambda e, hh=hh, h=h: e.tensor_tensor(out=bias[:, hh, :, :], in0=v3(csp)[:, :, h:h + 1].to_broadcast([128, 16, 4]),
                                                            in1=v4(offsp)[:, :, 0, h].unsqueeze(1).to_broadcast([128, 16, 4]), op=ALU.subtract),
                      reads=[b_scr], writes=[b_bias])
                d4 = lambda a: a.rearrange("p (I i) -> p I i", I=4)
                P.dve(lambda e, hh=hh, h=h: e.tensor_tensor(out=d4(dsh[:, hh, :]), in0=v4(offsp)[:, :, :, h],
                                                            in1=v4(offsp)[:, :, 0:1, h].to_broadcast([128, 4, 4]), op=ALU.subtract),
                      reads=[b_scr], writes=[b_sh])
                P.dve(lambda e, hh=hh: e.tensor_scalar(out=hi16[:, hh, :], in0=dsh[:, hh, :], scalar1=-8.0, scalar2=None, op0=ALU.mult),
                      reads=[b_sh], writes=[b_sh])
                P.dve(lambda e, hh=hh: e.scalar_tensor_tensor(out=rsh[:, hh, :], in0=dsh[:, hh, :], scalar=-8.0, in1=hi16[:, hh, :],
                                                              op0=ALU.mult, op1=ALU.subtract), reads=[b_sh], writes=[b_sh])
                P.dve(lambda e, hh=hh: e.tensor_scalar(out=v1[:, hh, :], in0=hi16[:, hh, :], scalar1=cf32[:, 256:257], scalar2=None, op0=ALU.mult),
                      reads=[b_sh, b_const], writes=[b_sh])
                P.dve(lambda e, hh=hh: e.scalar_tensor_tensor(out=valb[:, hh, :], in0=rsh[:, hh, :], scalar=cf32[:, 257:258], in1=v1[:, hh, :],
                                                              op0=ALU.mult, op1=ALU.add), reads=[b_sh, b_const], writes=[b_sh])
                oth = slice(64, 128) if hh == 0 else slice(0, 64)
                P.dve(lambda e, hh=hh, oth=oth: e.tensor_copy(out=qTm[hh][oth, :].rearrange("p (a b) -> p a b", a=16),
                                                               in_=valb[oth, hh, :].unsqueeze(2).to_broadcast([64, 16, 128])),
                      reads=[b_sh], writes=b_q)
                if c == 0:
                    P.act(lambda e, hh=hh, oth=oth: e.activation(out=kTm[hh][oth, :], in_=cf32[oth, 258:259].to_broadcast([64, T]), func=AF.Identity),
                          reads=[b_const], writes=b_k)
            P.pool(lambda e: e.memset(Va[:, :, :, 64:128], 1.0), writes=[b_va])
            for gg in range(4):
                bank = RR.next()
                for b in range(4):
                    blk = gg * 4 + b
                    for kc in range(KC):
                        mm(psum[bank][:, b * 128:(b + 1) * 128], hT[:, kc, blk * 128:(blk + 1) * 128], wvv[:, kc, :], kc == 0, kc == KC - 1,
                           [wt_buf[tv], hbuf[gg]], [ps_buf[bank]])
                P.dve(lambda e, gg=gg, bank=bank: e.tensor_copy(out=Va[:, gg * 4:(gg + 1) * 4, :, 0:64],
                                                                in_=psum[bank][:, :].rearrange("p (a b c) -> p a b c", a=4, b=2)),
                      reads=[ps_buf[bank]], writes=[b_va])
            kcnt = [0]

            def setup_n(n):
                for which in range(2):
                    r = kcnt[0] % 2
                    kcnt[0] += 1
                    bank = RR.next()
                    if which == 0:
                        proj_fm(wq, tq, n, bank)
                    else:
                        proj_fm(wk, tk, n, bank)
                    qk_norm_fm(bank, sq_t[r], b_sq[r], rs_t[r], b_rs[r], bd64)
                    dstm, bdd, gcol = (qTm, b_q, PP_FQN) if which == 0 else (kTm, b_k, PP_FKN)
                    for hh in range(2):
                        rows = slice(hh * 64, hh * 64 + 64)
                        P.dve(lambda e, r=r, bank=bank, n=n, hh=hh, rows=rows, dstm=dstm, gcol=gcol: e.scalar_tensor_tensor(
                            out=dstm[hh][rows, gsl(n)], in0=psum[bank][rows, :], scalar=pp[rows, l * NPP + gcol:l * NPP + gcol + 1], in1=rs_t[r][rows, :],
                            op0=ALU.mult, op1=ALU.mult), reads=[ps_buf[bank], b_rs[r], b_pp], writes=[bdd[n]])

            setup_n(0)
            setup_n(1)
            its = [(hh, I, j) for hh in range(2) for I in range(NG) for j in range(4 * I + 4)]
            LA = 2
            st = {}

            def emit_s(k):
                hh, I, j = its[k]
                t0 = max(j, 4 * I)
                col0 = (t0 - 4 * I) * 128
                diag = j >= 4 * I
                bS = RR.next()
                mm(psum[bS][:, col0:512], kTm[hh][:, j * 128:(j + 1) * 128], qTm[hh][:, I * 512 + col0:(I + 1) * 512], True, not diag,
                   [b_k[j // 4], b_q[I]], [ps_buf[bS]])
                if diag:
                    mm(psum[bS][:, col0:col0 + 128], ident, negU, False, True, [b_const], [ps_buf[bS]])
                p = k % NPT
                P.act(lambda e, p=p, bS=bS, col0=col0, hh=hh, j=j, I=I: e.activation(out=Pt[p][:, col0:512], in_=psum[bS][:, col0:512], func=AF.Exp, scale=0.125,
                                                                                    bias=bias[:, hh, j, I:I + 1]),
                      reads=[ps_buf[bS], b_bias], writes=[b_Pt[p]])

            def emit_pv(k):
                hh, I, j = its[k]
                nj = 4 * I + 4
                if j == 0:
                    st["bO"] = AR.next()
                bO = st["bO"]
                t0 = max(j, 4 * I)
                col0 = (t0 - 4 * I) * 128
                p = k % NPT
                mm(psum[bO][:, col0:512], Va[:, j, hh, :], Pt[p][:, col0:512], j == 0, j == nj - 1, [b_va, b_Pt[p]], [ps_buf[bO]])
                if j == nj - 1:
                    rows = slice(hh * 64, hh * 64 + 64)
                    r = I % 2
                    P.dve(lambda e, r=r, bO=bO: e.reciprocal(out=rc[r][0:64, :], in_=psum[bO][64:128, :]), reads=[ps_buf[bO]], writes=[b_rc[r]])
                    P.dve(lambda e, r=r, bO=bO, rows=rows, I=I: e.tensor_tensor(out=yT[rows, c, gsl(I)], in0=psum[bO][0:64, :], in1=rc[r][0:64, :], op=ALU.mult),
                          reads=[ps_buf[bO], b_rc[r]], writes=[ybuf[c][I]])

            for k in range(len(its) + LA):
                if k < len(its):
                    if its[k] == (0, 0, 0):
                        setup_n(2)
                    elif its[k] == (0, 1, 0):
                        setup_n(3)
                    emit_s(k)
                if k >= LA:
                    emit_pv(k - LA)

        for c in range(4):
            pair(c)
        P.barrier()

    def merge_phase(l):
        S.reset()
        mg = [S.take([KC, 512], BF16) for _ in range(2)]
        b_mg = bufs(2)
        sg = [S.take([512], F32) for _ in range(3)]
        b_sg = bufs(3)
        accm = [S.take([512], F32) for _ in range(2)]
        tmpm = [S.take([512], F32) for _ in range(2)]
        b_accm, b_tmpm = bufs(2), bufs(2)
        si = 0
        k = 0
        for n in range(NG):
            mgn, b_mgn = mg[n % 2], b_mg[n % 2]
            for m in range(KC):
                r = k % 2
                k += 1
                for b in range(3):
                    tg, wg = load_w("w_in", l, O_GL + b * D + m * 128, 128)
                    tb, wb = load_w("w_br", l, m * 128, 128, kc0=4 * b, nkc=4)
                    bG = RR.next()
                    proj_fm(wg, tg, n, bG)
                    s_ = si % 3
                    si += 1
                    P.act(lambda e, s_=s_, bG=bG: e.activation(out=sg[s_], in_=psum[bG][:, :], func=AF.Sigmoid), reads=[ps_buf[bG]], writes=[b_sg[s_]])
                    bP = RR.next()
                    for kc in range(4):
                        mm(psum[bP][:, :], wb[:, kc, :], yT[:, 4 * b + kc, gsl(n)], kc == 0, kc == 3,
                           [wt_buf[tb], ybuf[4 * b + kc][n]], [ps_buf[bP]])
                    if b == 0:
                        P.dve(lambda e, s_=s_, bP=bP, r=r: e.tensor_tensor(out=accm[r], in0=psum[bP][:, :], in1=sg[s_], op=ALU.mult),
                              reads=[ps_buf[bP], b_sg[s_]], writes=[b_accm[r]])
                    else:
                        P.dve(lambda e, s_=s_, bP=bP, r=r: e.tensor_tensor(out=tmpm[r], in0=psum[bP][:, :], in1=sg[s_], op=ALU.mult),
                              reads=[ps_buf[bP], b_sg[s_]], writes=[b_tmpm[r]])
                        if b == 1:
                            P.pool(lambda e, r=r: e.tensor_tensor(out=accm[r], in0=accm[r], in1=tmpm[r], op=ALU.add),
                                   reads=[b_accm[r], b_tmpm[r]], writes=[b_accm[r]])
                        else:
                            P.pool(lambda e, r=r, m=m, mgn=mgn: e.tensor_tensor(out=mgn[:, m, :], in0=accm[r], in1=tmpm[r], op=ALU.add),
                                   reads=[b_accm[r], b_tmpm[r]], writes=[b_mgn])
            for m in range(KC):
                to, wo = load_w("w_out", l, m * 128, 128)
                bank = AR.next()
                for kc in range(KC):
                    mm(psum[bank][:, :], wo[:, kc, :], mgn[:, kc, :], kc == 0, kc == KC - 1, [wt_buf[to], b_mgn], [ps_buf[bank]])
                P.dve(lambda e, m=m, n=n, bank=bank: e.tensor_tensor(out=xT[:, m, gsl(n)], in0=psum[bank][:, :], in1=xT[:, m, gsl(n)], op=ALU.add),
                      reads=[ps_buf[bank], xbuf[n]], writes=[xbuf[n]])
        P.barrier()

    def mlp_phase(l):
        S.reset()
        XS = Scratch(yT[:, :, :].rearrange("p a t -> p (a t)"))
        actT = XS.take([32, 512], BF16)
        b_act = bufs(32)
        rl = [S.take([512], F32) for _ in range(3)]
        b_rl = bufs(3)
        ri = 0
        for n in range(NG):
            for f in range(32):
                tu, wu = load_w("w_up", l, f * 128, 128)
                bank = RR.next()
                proj_fm(wu, tu, n, bank)
                r = ri % 3
                ri += 1
                P.act(lambda e, r=r, bank=bank: e.activation(out=rl[r], in_=psum[bank][:, :], func=AF.Relu), reads=[ps_buf[bank]], writes=[b_rl[r]])
                P.pool(lambda e, r=r, f=f: e.tensor_tensor(out=actT[:, f, :], in0=rl[r], in1=rl[r], op=ALU.mult), reads=[b_rl[r]], writes=[b_act[f]])
            for m in range(KC):
                bank = AR.next()
                for qd in range(4):
                    td, wd = load_w("w_dn", l, m * 128, 128, kc0=qd * 8, nkc=8)
                    for kc in range(8):
                        kk = qd * 8 + kc
                        mm(psum[bank][:, :], wd[:, kc, :], actT[:, kk, :], kk == 0, kk == 31, [wt_buf[td], b_act[kk]], [ps_buf[bank]])
                P.dve(lambda e, m=m, n=n, bank=bank: e.tensor_tensor(out=xT[:, m, gsl(n)], in0=psum[bank][:, :], in1=xT[:, m, gsl(n)], op=ALU.add),
                      reads=[ps_buf[bank], xbuf[n]], writes=[xbuf[n]])
        P.barrier()

    final = []
    for s in range(n_seq):
        for kc in range(KC):
            P.dma(lambda e, s=s, kc=kc: e.dma_start(out=xT[:, kc, :], in_=xT_d[s, kc * 128:(kc + 1) * 128, :]), writes=xbuf)
        for l in range(depth):
            rmsnorm_x(l, PP_GMIX)
            csp, offsp = gates_phase(l)
            if "M" in phases:
                mlstm_phase(l)
            if "S" in phases:
                swa_phase(l)
            if "F" in phases:
                fox_phase(l, csp, offsp)
            if dbg and l == 0 and s == 0:
                for c in range(12):
                    final.append(P.dma(lambda e, c=c: e.dma_start(out=dbg_y[c], in_=yT[:, c, :]), reads=ybuf[c]))
            if "G" in phases:
                merge_phase(l)
            if dbg and l == 0 and s == 0:
                for kc in range(KC):
                    final.append(P.dma(lambda e, kc=kc: e.dma_start(out=dbg_x1[kc], in_=xT[:, kc, :]), reads=xbuf))
            if "P" in phases:
                rmsnorm_x(l, PP_GMLP)
                mlp_phase(l)
            if l == depth - 1 or s > 0:
                flush_casts()
        for kc in range(KC):
            final.append(P.dma(lambda e, s=s, kc=kc: e.dma_start(out=yT_d[s, kc * 128:(kc + 1) * 128, :], in_=xT[:, kc, :]), reads=xbuf))
    P.emit(final_ops=final)
    return nc


def _consts():
    bf = ml_dtypes.bfloat16
    s = np.arange(128)[:, None]
    t = np.arange(128)[None, :]
    ident = (s == t).astype(np.float32)
    mU = (s <= t).astype(np.float32)
    mL = (s > t).astype(np.float32)
    bd = ((s // 64) == (t // 64)).astype(np.float32) / 64.0
    on128 = np.full((128, 128), 1.0 / 128.0, np.float32)
    rperm = ((s // 64 == t // 64) & ((s % 64) == ((t % 64) + 32) % 64)).astype(np.float32)
    on1024 = np.full((128, 128), 1.0 / 1024.0, np.float32)
    sel2 = np.zeros((128, 128), np.float32)
    sel2[0, :] = 1.0
    sel2[32, :] = 1.0
    NEG = -30000.0
    negU = np.where(s <= t, 0.0, NEG).astype(np.float32)
    negL = np.where(s > t, 0.0, NEG).astype(np.float32)
    pad = np.zeros((128, 128), np.float32)
    cmat = np.concatenate([ident, mU, mL, bd, on128, rperm, on1024, sel2, negU, negL, pad], axis=1).astype(bf)
    pcol = np.arange(128)
    selc = np.stack([(pcol % 64 == 0), (pcol % 64 == 32), (pcol % 32 == 0), np.zeros(128, bool)], axis=1).astype(np.float32)
    cf32 = np.concatenate([mU, np.ones((128, 128), np.float32), selc], axis=1).astype(np.float32)
    inv = (10000.0 ** (-np.arange(0, 64, 2, dtype=np.float32) / np.float32(64))).astype(np.float32)
    ang = np.arange(T, dtype=np.float32)[:, None] * inv[None, :]
    cos, sin = np.cos(ang).astype(np.float32), np.sin(ang).astype(np.float32)
    r = np.arange(128)
    cosT = cos[:, r % 32].T.copy()
    sgn = np.where((r % 64) < 32, -1.0, 1.0).astype(np.float32)
    sinT = (sin[:, r % 32].T * sgn[:, None]).astype(np.float32)
    rope = np.stack([cosT, sinT]).astype(np.float32)
    return cmat, cf32, rope


def _prep_params(inp, depth):
    pp = np.zeros((128, depth * NPP), np.float32)
    brow = np.zeros((128, depth * 16), np.float32)
    p = np.arange(128)
    for l in range(depth):
        o = l * NPP
        pp[:, o + PP_GMIX:o + PP_GMIX + 8] = inp["norm_mix"][l].reshape(8, 128).T
        pp[:, o + PP_GMLP:o + PP_GMLP + 8] = inp["norm_mlp"][l].reshape(8, 128).T
        pp[:, o + PP_FQN] = inp["fox_q_norm"][l][p % 64]
        pp[:, o + PP_FKN] = inp["fox_k_norm"][l][p % 64]
        pp[:, o + PP_SQN] = inp["swa_q_norm"][l][p % 64]
        pp[:, o + PP_SKN] = inp["swa_k_norm"][l][p % 64]
        pp[:, o + PP_SQNS] = inp["swa_q_norm"][l][(p % 64 + 32) % 64]
        pp[:, o + PP_SKNS] = inp["swa_k_norm"][l][(p % 64 + 32) % 64]
        cw = inp["conv_w"][l]
        for cc in range(8):
            for j in range(4):
                pp[:, o + PP_CONVW + cc * 4 + j] = cw[j, cc * 128:(cc + 1) * 128]
            pp[:, o + PP_CONVB + cc] = inp["conv_b"][l][cc * 128:(cc + 1) * 128]
        pp[:, o + PP_MON:o + PP_MON + 4] = inp["mlstm_out_norm"][l].reshape(4, 128).T
        pp[:, o + PP_MFB:o + PP_MFB + 4] = inp["mlstm_f_bias"][l][None, :]
        pp[:, o + PP_SINK:o + PP_SINK + 8] = inp["swa_sinks"][l][None, :]
        brow[:, l * 16:l * 16 + 8] = inp["fox_f_bias"][l][None, :]
        brow[:, l * 16 + 8:l * 16 + 12] = inp["mlstm_i_bias"][l][None, :]
        brow[:, l * 16 + 12:l * 16 + 16] = inp["mlstm_f_bias"][l][None, :]
    return pp, brow


def _prep_waux(w_in):
    depth = w_in.shape[0]
    aux = np.empty((depth, D, NAUX), np.float32)
    for g in range(2):
        k = w_in[:, :, O_SK + g * 64:O_SK + (g + 1) * 64]
        aux[:, :, g * 128:g * 128 + 64] = k
        aux[:, :, g * 128 + 64:g * 128 + 128] = k
    for h in range(4):
        aux[:, :, 256 + h * 128:256 + (h + 1) * 128] = w_in[:, :, O_MF + h:O_MF + h + 1]
    aux[:, :, 768:776] = w_in[:, :, O_FF:O_FF + 8]
    aux[:, :, 776:780] = w_in[:, :, O_MI:O_MI + 4]
    aux[:, :, 780:784] = w_in[:, :, O_MF:O_MF + 4]
    return aux


_NC_CACHE = {}


def _get_nc(n_seq, depth, dbg=False, phases="MSFGP"):
    key = (n_seq, depth, dbg, phases)
    if key not in _NC_CACHE:
        _NC_CACHE[key] = build(n_seq, depth, dbg, phases)
    return _NC_CACHE[key]


def make_maps(inp, n_cores, n_seq, depth):
    f = lambda a: np.ascontiguousarray(np.asarray(a, dtype=np.float32))
    cmat, cf32, rope = _consts()
    pp, brow = _prep_params(inp, depth)
    w_in = f(inp["w_in"][:depth])
    shared = {
        "w_in": w_in, "w_aux": _prep_waux(w_in), "w_br": f(inp["w_branch"][:depth]).reshape(depth, 1536, D),
        "w_out": f(inp["w_out"][:depth]), "w_up": f(inp["w_up"][:depth]), "w_dn": f(inp["w_down"][:depth]),
        "pp": pp, "brow": brow, "rope": rope, "cmat": cmat, "cf32": cf32,
    }
    x = np.asarray(inp["x"], dtype=np.float32)
    maps = []
    for c in range(n_cores):
        xs = x[c * n_seq:(c + 1) * n_seq]
        m = dict(shared)
        m["xT"] = np.ascontiguousarray(xs.transpose(0, 2, 1))
        maps.append(m)
    return maps


def kernel(**inputs):
    n_seq = 32 // NCORES
    nc = _get_nc(n_seq, DEPTH)
    maps = make_maps(inputs, NCORES, n_seq, DEPTH)
    res = run_bass_kernel_spmd(nc, maps, core_ids=list(range(NCORES)))
    outs = [np.asarray(r["yT"]).transpose(0, 2, 1) for r in res.results]
    return np.ascontiguousarray(np.concatenate(outs, axis=0).astype(np.float32))
```

```python
import math
import numpy as np
import ml_dtypes
import concourse.bass as bass
import concourse.mybir as mybir
from concourse.bass_utils import run_bass_kernel_spmd

F32 = mybir.dt.float32
BF16 = mybir.dt.bfloat16
AF = mybir.ActivationFunctionType
ALU = mybir.AluOpType

D = 1024
T = 2048
KC = 8
NG = 4
NB = 16
DEPTH = 2
NCORES = 8
IN_W = 7440
NAUX = 784
EPS = 1e-6
O_FQ, O_FK, O_FV, O_FF = 0, 512, 1024, 1536
O_SQ, O_SK, O_SV = 1544, 2056, 2184
O_MQ, O_MK, O_MV, O_MI, O_MF, O_MO, O_GL = 2312, 2824, 3336, 3848, 3852, 3856, 4368
PP_GMIX, PP_GMLP = 0, 8
PP_FQN, PP_FKN, PP_SQN, PP_SKN, PP_SQNS, PP_SKNS = 16, 17, 18, 19, 20, 21
PP_CONVW, PP_CONVB = 22, 54
PP_MON, PP_MFB, PP_SINK = 62, 66, 70
NPP = 78
NDMASEM = 12
COMPUTE = ("pe", "dve", "act", "pool")


class Buf:
    __slots__ = ("lw", "rd")

    def __init__(self):
        self.lw = None
        self.rd = []


def bufs(*shape):
    if len(shape) == 1:
        return [Buf() for _ in range(shape[0])]
    return [bufs(*shape[1:]) for _ in range(shape[0])]


class BufCache:
    def __init__(self):
        self.objs = []
        self.i = 0

    def rewind(self):
        self.i = 0

    def buf(self):
        if self.i == len(self.objs):
            self.objs.append(Buf())
        b = self.objs[self.i]
        self.i += 1
        return b

    def bufs(self, *shape):
        if len(shape) == 1:
            return [self.buf() for _ in range(shape[0])]
        return [self.bufs(*shape[1:]) for _ in range(shape[0])]


class Op:
    __slots__ = ("eng", "fn", "deps", "sig", "sem", "val", "is_dma", "prev_val")

    def __init__(self, eng, fn, is_dma):
        self.eng = eng
        self.fn = fn
        self.deps = []
        self.sig = is_dma
        self.sem = None
        self.val = 0
        self.prev_val = 0
        self.is_dma = is_dma


class Prog:
    def __init__(self, nc):
        self.nc = nc
        self.ops = {e: [] for e in ("pe", "dve", "act", "pool", "sp")}
        self.pending = {e: [] for e in self.ops}

    def barrier(self):
        last = [lst[-1] for e, lst in self.ops.items() if e in COMPUTE and lst]
        for e in self.pending:
            if e != "pe":
                self.pending[e] = list(last)

    def add(self, eng, fn, reads=(), writes=(), dma=False, skip_barrier=False):
        op = Op(eng, fn, dma)
        deps = {}
        for b in reads:
            if b.lw is not None:
                deps[id(b.lw)] = (b.lw, True)
        for b in writes:
            if b.lw is not None and id(b.lw) not in deps:
                deps[id(b.lw)] = (b.lw, False)
            for r in b.rd:
                if id(r) not in deps:
                    deps[id(r)] = (r, False)
        if not skip_barrier:
            for d in self.pending[eng]:
                if id(d) not in deps:
                    deps[id(d)] = (d, True)
            self.pending[eng] = []
        for d, raw in deps.values():
            if d is op:
                continue
            if (not d.is_dma) and (not dma) and d.eng == eng:
                if eng == "pe":
                    continue
            op.deps.append(d)
            d.sig = True
        for b in reads:
            b.rd.append(op)
        for b in writes:
            b.lw = op
            b.rd = []
        self.ops[eng].append(op)
        return op

    def pe(self, fn, reads=(), writes=()):
        return self.add("pe", fn, reads, writes)

    def dve(self, fn, reads=(), writes=()):
        return self.add("dve", fn, reads, writes)

    def act(self, fn, reads=(), writes=()):
        return self.add("act", fn, reads, writes)

    def pool(self, fn, reads=(), writes=()):
        return self.add("pool", fn, reads, writes)

    def dma(self, fn, reads=(), writes=(), q="sp", skip_barrier=False):
        return self.add(q, fn, reads, writes, dma=True, skip_barrier=skip_barrier)

    def emit(self, final_ops=()):
        nc = self.nc
        esem = {e: nc.alloc_semaphore("s_" + e) for e in COMPUTE}
        for e, lst in self.ops.items():
            c = 0
            for op in lst:
                if op.is_dma:
                    continue
                if op.sig:
                    c += 1
                    op.sem = esem[e]
                    op.val = c
        for e, lst in self.ops.items():
            if not any(op.is_dma for op in lst):
                continue
            nds = 40 if e == "pool" else NDMASEM
            dsem = [nc.alloc_semaphore("s_dma_%s%d" % (e, i)) for i in range(nds)]
            dcnt = [0] * nds
            di = 0
            for op in lst:
                if op.is_dma:
                    k = di % nds
                    di += 1
                    op.prev_val = dcnt[k]
                    dcnt[k] += 16
                    op.sem = dsem[k]
                    op.val = dcnt[k]
        final_ops = list(final_ops)

        def run(ename, eng):
            known = {}
            for op in self.ops[ename]:
                need = {}
                for d in op.deps:
                    key = id(d.sem)
                    if key not in need or need[key][1] < d.val:
                        need[key] = (d.sem, d.val)
                if op.is_dma and op.prev_val > 0:
                    key = id(op.sem)
                    if key not in need or need[key][1] < op.prev_val:
                        need[key] = (op.sem, op.prev_val)
                for key, (sem, val) in need.items():
                    if known.get(key, 0) >= val:
                        continue
                    eng.wait_ge(sem, val)
                    known[key] = val
                ins = op.fn(eng)
                if op.is_dma:
                    ins.then_inc(op.sem, 16)
                elif op.sig:
                    ins.then_inc(op.sem, 1)
            if ename == "sp":
                for op in final_ops:
                    key = id(op.sem)
                    if known.get(key, 0) < op.val:
                        eng.wait_ge(op.sem, op.val)
                        known[key] = op.val

        with nc.Block() as block:
            @block.sync
            def _(e):
                run("sp", e)

            @block.tensor
            def _(e):
                run("pe", e)

            @block.vector
            def _(e):
                run("dve", e)

            @block.scalar
            def _(e):
                run("act", e)

            @block.gpsimd
            def _(e):
                run("pool", e)


class Ring:
    def __init__(self, items):
        self.items = items
        self.i = 0

    def next(self):
        it = self.items[self.i % len(self.items)]
        self.i += 1
        return it


class Scratch:
    def __init__(self, ap2d):
        self.ap = ap2d
        self.n = ap2d.shape[1]
        self.off = 0

    def reset(self):
        self.off = 0

    def take(self, shape, dtype):
        n = int(np.prod(shape))
        nb = n * (2 if dtype == F32 else 1)
        nb = (nb + 15) // 16 * 16
        assert self.off + nb <= self.n, ("scratch overflow", self.off, nb, self.n)
        v = self.ap[:, self.off:self.off + n * (2 if dtype == F32 else 1)]
        self.off += nb
        if dtype == F32:
            v = v.bitcast(F32)
        if len(shape) == 2:
            v = v.rearrange("p (a b) -> p a b", a=shape[0])
        elif len(shape) == 3:
            v = v.rearrange("p (a b c) -> p a b c", a=shape[0], b=shape[1])
        return v


def build(n_seq, depth, dbg=False, phases="MSFGP"):
    nc = bass.Bass("TRN2", target_bir_lowering=False)
    P = Prog(nc)
    dten = nc.dram_tensor
    xT_d = dten("xT", [n_seq, D, T], F32, kind="ExternalInput").ap()
    w_in_d = dten("w_in", [depth, D, IN_W], F32, kind="ExternalInput").ap()
    w_aux_d = dten("w_aux", [depth, D, NAUX], F32, kind="ExternalInput").ap()
    w_br_d = dten("w_br", [depth, 1536, D], F32, kind="ExternalInput").ap()
    w_out_d = dten("w_out", [depth, D, D], F32, kind="ExternalInput").ap()
    w_up_d = dten("w_up", [depth, D, 4 * D], F32, kind="ExternalInput").ap()
    w_dn_d = dten("w_dn", [depth, 4 * D, D], F32, kind="ExternalInput").ap()
    pp_d = dten("pp", [128, depth * NPP], F32, kind="ExternalInput").ap()
    brow_d = dten("brow", [128, depth * 16], F32, kind="ExternalInput").ap()
    rope_d = dten("rope", [2, 128, T], F32, kind="ExternalInput").ap()
    cmat_d = dten("cmat", [128, 11 * 128], BF16, kind="ExternalInput").ap()
    cf32_d = dten("cf32", [128, 260], F32, kind="ExternalInput").ap()
    yT_d = dten("yT", [n_seq, D, T], F32, kind="ExternalOutput").ap()
    if dbg:
        dbg_y = dten("dbg_y", [12, 128, T], BF16, kind="ExternalOutput").ap()
        dbg_x1 = dten("dbg_x1", [8, 128, T], F32, kind="ExternalOutput").ap()
    wsc = {}
    wsc_buf = {}
    specs = {"w_in": (w_in_d, D, IN_W), "w_aux": (w_aux_d, D, NAUX), "w_br": (w_br_d, 1536, D),
             "w_out": (w_out_d, D, D), "w_up": (w_up_d, D, 4 * D), "w_dn": (w_dn_d, 4 * D, D)}
    PR = 256
    for name, (src, rows, cols) in specs.items():
        for l in range(depth):
            wsc[(name, l)] = dten("sc_%s_%d" % (name, l), [rows, cols], BF16).ap()
            wsc_buf[(name, l)] = bufs(rows // 128)

    lazy_casts = []

    def flush_casts(key=None):
        if key is not None and not any(k == key for k, _ in lazy_casts):
            return
        while lazy_casts:
            k, fn_ = lazy_casts.pop(0)
            fn_()
            if key is not None and not any(k2 == key for k2, _ in lazy_casts):
                return

    def cast_tick(every=[0]):
        every[0] += 1
        if lazy_casts and every[0] % 2 == 0:
            lazy_casts.pop(0)[1]()

    def cast_weights(l, lazy=False, names=("w_aux", "w_in", "w_br", "w_out", "w_up", "w_dn")):
        for name in names:
            src, rows, cols = specs[name]
            t = wsc[(name, l)]
            bl = wsc_buf[(name, l)]
            pr = 128 if lazy else PR
            for r in range(rows // pr):
                def issue(t=t, src=src, r=r, pr=pr, bl=bl):
                    P.dma(lambda e: e.dma_start(out=t[r * pr:(r + 1) * pr, :], in_=src[l, r * pr:(r + 1) * pr, :]),
                          writes=bl[r * (pr // 128):(r + 1) * (pr // 128)], q="pool")
                if lazy:
                    lazy_casts.append(((name, l), issue))
                else:
                    issue()

    cast_weights(0, names=("w_aux", "w_in"))
    cast_weights(0, lazy=True, names=("w_br", "w_out", "w_up", "w_dn"))
    for l_ in range(1, depth):
        cast_weights(l_, lazy=True)

    sb = nc.alloc_sbuf_tensor
    xT = sb("xT_sb", [128, KC, T], F32)
    hT = sb("hT_sb", [128, KC, T], BF16)
    yT = sb("yT_sb", [128, 12, T], BF16)
    NW = 8
    wt = [sb("wt%d" % i, [128, 1024], BF16) for i in range(NW)]
    wt_buf = bufs(NW)
    wring = Ring(list(range(NW)))
    scr_t = sb("scr", [128, 18432], BF16)
    cmat = sb("cmat_sb", [128, 11 * 128], BF16)
    cf32 = sb("cf32_sb", [128, 260], F32)
    pp = sb("pp_sb", [128, depth * NPP], F32)
    brow = sb("brow_sb", [128, depth * 16], F32)
    wsm = sb("wsm_sb", [128, KC, 16], BF16)
    gsm = sb("gsm_sb", [128, 2, 256], F32)
    wexp = sb("wexp_sb", [128, 16, 4], F32)
    esink = sb("esink_sb", [128, 8], F32)
    nmfb = sb("nmfb_sb", [128, 4], F32)
    ident = cmat[:, 0:128]
    maskUL = cmat[:, 128:384]
    maskU = cmat[:, 128:256]
    maskL = cmat[:, 256:384]
    bd64 = cmat[:, 384:512]
    on128 = cmat[:, 512:640]
    rperm = cmat[:, 640:768]
    on1024 = cmat[:, 768:896]
    sel2 = cmat[:, 896:1024]
    negUL = cmat[:, 1024:1280]
    negU = cmat[:, 1024:1152]
    negL = cmat[:, 1152:1280]
    triU32 = cf32[:, 0:128]
    ones32 = cf32[:, 128:256]
    b_const = Buf()
    b_pp = Buf()
    P.dma(lambda e: e.dma_start(out=cmat[:], in_=cmat_d), writes=[b_const])
    P.dma(lambda e: e.dma_start(out=cf32[:], in_=cf32_d), writes=[b_const])
    P.dma(lambda e: e.dma_start(out=pp[:], in_=pp_d), writes=[b_pp])
    P.dma(lambda e: e.dma_start(out=brow[:], in_=brow_d), writes=[b_pp])

    psum = [nc.alloc_psum_tensor("ps%d" % i, [128, 512], F32) for i in range(8)]
    ps_buf = bufs(8)
    RR = Ring([0, 1, 2, 3])
    AR = Ring([4, 5, 6, 7])

    xbuf = bufs(NG)
    hbuf = bufs(NG)
    ybuf = bufs(12, NG)
    b_scr = Buf()
    S = Scratch(scr_t[:, :])

    def gsl(n):
        return slice(n * 512, (n + 1) * 512)

    def load_w(name, l, c0, ncols, kc0=0, nkc=KC, dst_off=0, tile=None):
        if tile is None:
            tile = wring.next()
        flush_casts((name, l))
        cast_tick()
        src = wsc[(name, l)][kc0 * 128:(kc0 + nkc) * 128, c0:c0 + ncols].rearrange("(kc p) c -> p kc c", p=128)
        dst = wt[tile][:, dst_off:dst_off + nkc * ncols].rearrange("p (kc c) -> p kc c", kc=nkc)
        P.dma(lambda e: e.dma_start(out=dst, in_=src), reads=wsc_buf[(name, l)][kc0:kc0 + nkc], writes=[wt_buf[tile]], skip_barrier=True)
        return tile, dst

    def mm(out, lhsT, rhs, start, stop, reads, writes, skip=False):
        if skip:
            P.pe(lambda e: e.matmul(out, lhsT=lhsT, rhs=rhs, start=start, stop=stop, skip_group_check=True), reads=reads, writes=writes)
        else:
            P.pe(lambda e: e.matmul(out, lhsT=lhsT, rhs=rhs, start=start, stop=stop), reads=reads, writes=writes)

    def proj_fm(wv, tile, n, bank, ncol=128, c0=0):
        cast_tick()
        for kc in range(KC):
            mm(psum[bank][0:ncol, :], wv[:, kc, c0:c0 + ncol], hT[:, kc, gsl(n)], kc == 0, kc == KC - 1,
               [wt_buf[tile], hbuf[n]], [ps_buf[bank]])

    def rstd_from(bank_stats, rstd_ap, wbuf, half=False):
        P.act(lambda e: e.activation(out=rstd_ap, in_=psum[bank_stats][:, :], func=AF.Ln, bias=EPS),
              reads=[ps_buf[bank_stats]], writes=[wbuf])
        if half:
            P.act(lambda e: e.activation(out=rstd_ap, in_=rstd_ap, func=AF.Exp, scale=-0.5, bias=math.log(0.5)), reads=[wbuf], writes=[wbuf])
        else:
            P.act(lambda e: e.activation(out=rstd_ap, in_=rstd_ap, func=AF.Exp, scale=-0.5), reads=[wbuf], writes=[wbuf])

    def rmsnorm_x(l, gcol0):
        S.reset()
        sq = [S.take([512], BF16) for _ in range(4)]
        sqb = bufs(4)
        rs = [S.take([512], F32) for _ in range(2)]
        rsb = bufs(2)
        for n in range(NG):
            bank = RR.next()
            for kc in range(KC):
                i = kc % 4
                fn = (lambda e, i=i, kc=kc, n=n: e.activation(out=sq[i], in_=xT[:, kc, gsl(n)], func=AF.Square))
                if kc % 2 == 0:
                    P.act(fn, reads=[xbuf[n]], writes=[sqb[i]])
                else:
                    P.pool(lambda e, i=i, kc=kc, n=n: e.tensor_tensor(out=sq[i], in0=xT[:, kc, gsl(n)], in1=xT[:, kc, gsl(n)], op=ALU.mult),
                           reads=[xbuf[n]], writes=[sqb[i]])
                mm(psum[bank][:, :], on1024, sq[i], kc == 0, kc == KC - 1, [b_const, sqb[i]], [ps_buf[bank]])
            r = n % 2
            rstd_from(bank, rs[r], rsb[r])
            for kc in range(KC):
                P.dve(lambda e, kc=kc, n=n, r=r: e.scalar_tensor_tensor(
                    out=hT[:, kc, gsl(n)], in0=xT[:, kc, gsl(n)], scalar=pp[:, l * NPP + gcol0 + kc:l * NPP + gcol0 + kc + 1],
                    in1=rs[r], op0=ALU.mult, op1=ALU.mult), reads=[xbuf[n], rsb[r], b_pp], writes=[hbuf[n]])
        P.barrier()

    def gates_phase(l):
        S.reset()
        Gs = S.take([256], F32)
        sp = S.take([256], F32)
        totS = S.take([256], F32)
        incl = S.take([256], F32)
        offsp = gsm[:, 0, :]
        csp = gsm[:, 1, :]
        v3 = lambda a: a.rearrange("p (b c) -> p b c", b=16)
        flush_casts(("w_aux", l))
        P.dma(lambda e: e.dma_start(out=wsm[:], in_=wsc[("w_aux", l)][:, 768:784].rearrange("(kc p) c -> p kc c", p=128)),
              reads=wsc_buf[("w_aux", l)], writes=[b_scr])
        bank = RR.next()
        for b in range(NB):
            for kc in range(KC):
                mm(psum[bank][:, b * 16:(b + 1) * 16], hT[:, kc, b * 128:(b + 1) * 128], wsm[:, kc, :], kc == 0, kc == KC - 1,
                   [b_scr, hbuf[b // 4]], [ps_buf[bank]])
        P.dve(lambda e: e.tensor_tensor(out=v3(Gs), in0=v3(psum[bank][:, 0:256]),
                                        in1=brow[:, l * 16:(l + 1) * 16].unsqueeze(1).to_broadcast([128, 16, 16]), op=ALU.add),
              reads=[ps_buf[bank], b_pp], writes=[b_scr])
        P.act(lambda e: e.activation(out=sp, in_=Gs, func=AF.Exp, scale=-1.0), reads=[b_scr], writes=[b_scr])
        P.act(lambda e: e.activation(out=sp, in_=sp, func=AF.Ln, bias=1.0), reads=[b_scr], writes=[b_scr])
        bank2 = RR.next()
        mm(psum[bank2][:, 0:256], triU32, sp, True, True, [b_scr, b_const], [ps_buf[bank2]])
        mm(psum[bank2][:, 256:512], ones32, sp, True, True, [b_scr, b_const], [ps_buf[bank2]])
        P.dve(lambda e: e.tensor_copy(out=sp, in_=psum[bank2][:, 0:256]), reads=[ps_buf[bank2]], writes=[b_scr])
        P.dve(lambda e: e.tensor_copy(out=totS, in_=psum[bank2][:, 256:512]), reads=[ps_buf[bank2]], writes=[b_scr])
        for h in range(8):
            P.dve(lambda e, h=h: e.tensor_tensor_scan(out=v3(incl)[:, :, h], data0=ones32[:, 0:16], data1=v3(totS)[:, :, h],
                                                      initial=0.0, op0=ALU.mult, op1=ALU.add), reads=[b_scr, b_const], writes=[b_scr])
        P.dve(lambda e: e.tensor_tensor(out=offsp, in0=incl, in1=totS, op=ALU.subtract), reads=[b_scr], writes=[b_scr])
        P.dve(lambda e: e.tensor_tensor(out=csp, in0=sp, in1=offsp, op=ALU.add), reads=[b_scr], writes=[b_scr])
        P.dve(lambda e: e.tensor_tensor(out=wexp[:], in0=v3(Gs)[:, :, 8:12], in1=v3(sp)[:, :, 12:16], op=ALU.add), reads=[b_scr], writes=[b_scr])
        P.act(lambda e: e.activation(out=wexp[:], in_=wexp[:], func=AF.Exp, bias=-0.5 * math.log(128.0) + math.log(0.5)), reads=[b_scr], writes=[b_scr])
        P.act(lambda e: e.activation(out=esink[:], in_=pp[:, l * NPP + PP_SINK:l * NPP + PP_SINK + 8], func=AF.Exp), reads=[b_pp], writes=[b_scr])
        P.dve(lambda e: e.tensor_scalar(out=nmfb[:], in0=pp[:, l * NPP + PP_MFB:l * NPP + PP_MFB + 4], scalar1=0.5, scalar2=None, op0=ALU.mult),
              reads=[b_pp], writes=[b_scr])
        P.barrier()
        return csp, offsp

    def mlstm_phase(l):
        ppc = lambda c: pp[:, l * NPP + c:l * NPP + c + 1]
        BC = BufCache()

        def head(h):
            BC.rewind()
            S.reset()
            b_sgt_holder = [BC.buf()]
            XS = Scratch(yT[:, 0:8, :].rearrange("p a t -> p (a t)"))
            pre = XS.take([2064], F32)
            ebrep = XS.take([T], F32)
            qT = XS.take([T], BF16)
            kT = XS.take([T], BF16)
            Vp = XS.take([NB, 132], BF16)
            so = S.take([T], BF16)
            kTok = S.take([NB, 128], BF16)
            wrep = S.take([NB, 128], BF16)
            b_pre, b_eb, b_so, b_q, b_k, b_ktok, b_vp, b_wrep = BC.bufs(8)
            RG = S.take([6 * 512], F32)
            spn = [RG[:, i * 512:(i + 1) * 512] for i in (0, 1)]
            csn = [RG[:, i * 512:(i + 1) * 512] for i in (2, 3)]
            acc = [RG[:, i * 512:(i + 1) * 512] for i in (4, 5)]
            b_spn, b_csn, b_acc = BC.bufs(2), BC.bufs(2), BC.bufs(2)
            A3 = RG[:, 0:2064]
            b_A3 = [b_spn[0], b_spn[1], b_csn[0], b_csn[1], b_acc[0]]
            mS = [S.take([128], BF16) for _ in range(3)]
            b_mS = BC.bufs(3)
            Cb_all = S.take([NB, 128], BF16)
            b_Cb = BC.buf()
            a16 = S.take([16], F32)
            b_a16 = BC.buf()
            nrep_all = kTok
            d1 = [S.take([512], F32), XS.take([512], F32)]
            sgt = d1[0]
            b_sgt = b_sgt_holder[0]
            rsn = [S.take([512], F32), S.take([512], F32)]
            sqn = [S.take([512], BF16), XS.take([512], BF16)]
            hg = d1
            b_d1, b_sqn, b_rsn = BC.bufs(2), BC.bufs(2), BC.bufs(2)
            b_d1[0] = b_sgt_holder[0]
            b_hg = b_d1
            tf, wf = load_w("w_aux", l, 256 + h * 128, 128)
            to_, wo_ = load_w("w_in", l, O_MO + h * 128, 128)
            tq_, wq_ = load_w("w_in", l, O_MQ + h * 128, 128)
            tk_, wk_ = load_w("w_in", l, O_MK + h * 128, 128)
            tv_, wv_ = load_w("w_in", l, O_MV + h * 128, 128)
            win = [[pre[:, (2 * w + r) * 515:(2 * w + r + 1) * 515] for r in range(2)] for w in range(2)]
            b_win = BC.bufs(2, 2)
            b_allwin = [b_win[0][0], b_win[0][1], b_win[1][0], b_win[1][1]]
            P.dve(lambda e: e.tensor_copy(out=Vp[:, :, 128:129], in_=wexp[:, :, h:h + 1]), reads=[b_scr], writes=[b_vp])
            P.pool(lambda e: e.tensor_copy(out=wrep, in_=wexp[:, :, h:h + 1].to_broadcast([128, NB, 128])), reads=[b_scr], writes=[b_wrep])

            def transposes(g):
                bank = AR.next()
                pb = psum[bank][:, :].bitcast(BF16)
                for b_ in range(4):
                    blk = g * 4 + b_
                    P.pe(lambda e, pb=pb, b_=b_, blk=blk: e.transpose(pb[:, b_ * 128:(b_ + 1) * 128], kT[:, blk * 128:(blk + 1) * 128], ident),
                         reads=[b_k, b_const], writes=[ps_buf[bank]])
                P.act(lambda e, pb=pb, g=g: e.activation(out=kTok[:, g * 4:(g + 1) * 4, :], in_=pb[:, 0:512].rearrange("p (a b) -> p a b", a=4), func=AF.Identity),
                      reads=[ps_buf[bank]], writes=[b_ktok])

            for n in range(NG):
                r = n % 2
                bF = RR.next()
                proj_fm(wf, tf, n, bF)
                P.act(lambda e, r=r, bF=bF: e.activation(out=spn[r], in_=psum[bF][:, :], func=AF.Tanh, scale=0.5, bias=nmfb[:, h:h + 1]),
                      reads=[ps_buf[bF], b_scr], writes=[b_spn[r]])
                bO_ = RR.next()
                proj_fm(wo_, to_, n, bO_)
                P.act(lambda e, n=n, bO_=bO_: e.activation(out=so[:, gsl(n)], in_=psum[bO_][:, :], func=AF.Tanh, scale=0.5), reads=[ps_buf[bO_]], writes=[b_so])
                for w, (wv2, tt2) in enumerate(((wq_, tq_), (wk_, tk_))):
                    bQ = RR.next()
                    proj_fm(wv2, tt2, n, bQ)
                    P.act(lambda e, w=w, r=r, bQ=bQ: e.activation(out=win[w][r][:, 3:515], in_=psum[bQ][:, :], func=AF.Identity),
                          reads=[ps_buf[bQ]], writes=[b_win[w][r]])
                    if n == 0:
                        P.pool(lambda e, w=w, r=r: e.memset(win[w][r][:, 0:3], 0.0), writes=[b_win[w][r]])
                    else:
                        P.pool(lambda e, w=w, r=r: e.tensor_copy(out=win[w][r][:, 0:3], in_=win[w][1 - r][:, 512:515]),
                               reads=[b_win[w][1 - r]], writes=[b_win[w][r]])
                bank = AR.next()
                for b_ in range(4):
                    blk = n * 4 + b_
                    for kc in range(KC):
                        mm(psum[bank][:, b_ * 128:(b_ + 1) * 128], hT[:, kc, blk * 128:(blk + 1) * 128], wv_[:, kc, :], kc == 0, kc == KC - 1,
                           [wt_buf[tv_], hbuf[n]], [ps_buf[bank]])
                for b_ in range(4):
                    blk = n * 4 + b_
                    P.dve(lambda e, b_=b_, blk=blk, bank=bank: e.tensor_scalar(out=Vp[:, blk, 0:128], in0=psum[bank][:, b_ * 128:(b_ + 1) * 128],
                                                                               scalar1=wexp[:, blk, h:h + 1], scalar2=None, op0=ALU.mult),
                          reads=[ps_buf[bank], b_scr], writes=[b_vp])
                if n > 0:
                    transposes(n - 1)
                P.dve(lambda e, r=r: e.tensor_scalar(out=spn[r], in0=spn[r], scalar1=1.0, scalar2=0.5, op0=ALU.add, op1=ALU.mult),
                      reads=[b_spn[r]], writes=[b_spn[r]])
                for j in range(4):
                    P.dve(lambda e, r=r, j=j, n=n: e.tensor_tensor_scan(out=ebrep[:, n * 512 + j * 128:n * 512 + (j + 1) * 128], data0=spn[r][:, j * 128:(j + 1) * 128],
                                                                        data1=ones32, initial=1.0, op0=ALU.mult, op1=ALU.mult),
                          reads=[b_spn[r], b_const], writes=[b_eb])
                for w, (dst, b_dst) in enumerate(((qT, b_q), (kT, b_k))):
                    cc = w * 4 + h
                    cw = lambda j, cc=cc: ppc(PP_CONVW + cc * 4 + j)
                    P.act(lambda e, w=w, r=r, cw=cw, cc=cc: e.activation(out=acc[w], in_=win[w][r][:, 3:515], func=AF.Identity,
                                                                        scale=cw(3), bias=ppc(PP_CONVB + cc)),
                          reads=[b_win[w][r], b_pp], writes=[b_acc[w]])
                    for j in (2, 1, 0):
                        P.dve(lambda e, w=w, r=r, j=j, cw=cw: e.scalar_tensor_tensor(out=acc[w], in0=win[w][r][:, j:j + 512], scalar=cw(j),
                                                                                    in1=acc[w], op0=ALU.mult, op1=ALU.add),
                              reads=[b_win[w][r], b_acc[w], b_pp], writes=[b_acc[w]])
                    P.act(lambda e, w=w: e.activation(out=sgt, in_=acc[w], func=AF.Tanh, scale=0.5), reads=[b_acc[w]], writes=[b_sgt])
                    P.dve(lambda e, n=n, w=w, dst=dst: e.scalar_tensor_tensor(out=dst[:, gsl(n)], in0=sgt, scalar=1.0, in1=acc[w], op0=ALU.add, op1=ALU.mult),
                          reads=[b_acc[w], b_sgt], writes=[b_dst])
            transposes(NG - 1)
            b_pre_all = [b_pre] + b_allwin
            tU3 = pre.rearrange("p (e c) -> p e c", c=16)
            ebv = ebrep.rearrange("p (c t) -> p c t", t=128)
            P.dve(lambda e: e.tensor_copy(out=a16, in_=ebv[:, :, 127]), reads=[b_eb], writes=[b_a16])
            P.dve(lambda e: e.memset(a16[:, 0:1], 0.0), writes=[b_a16])
            for c in range(NB - 1):
                bU = RR.next()
                mm(psum[bU][:, 0:129], kTok[:, c, :], Vp[:, c, 0:129], True, True, [b_ktok, b_vp], [ps_buf[bU]])
                P.dve(lambda e, bU=bU, c=c: e.tensor_scalar(out=tU3[:, :, c], in0=psum[bU][:, 0:129], scalar1=ebrep[:, c * 128 + 127:c * 128 + 128],
                                                            scalar2=None, op0=ALU.mult), reads=[ps_buf[bU], b_eb], writes=b_pre_all)
            P.dve(lambda e: e.memset(tU3[:, :, 15:16], 0.0), writes=b_pre_all)
            P.act(lambda e: e.activation(out=A3.rearrange("p (e c) -> p e c", c=16), in_=a16.unsqueeze(1).to_broadcast([128, 129, 16]), func=AF.Identity),
                  reads=[b_a16], writes=b_A3)
            zeros128 = cmat[:, 1280:1408]
            gb = {}

            def emit_s(c):
                csl = slice(c * 128, (c + 1) * 128)
                bS = RR.next()
                mm(psum[bS][:, 0:128], kT[:, csl], qT[:, csl], True, True, [b_k, b_q], [ps_buf[bS]])
                mi = c % 3
                P.dve(lambda e, mi=mi, bS=bS: e.tensor_tensor(out=mS[mi], in0=psum[bS][:, 0:128], in1=maskU, op=ALU.mult),
                      reads=[ps_buf[bS], b_const], writes=[b_mS[mi]])

            def intra(g):
                bN1, bN2 = AR.next(), AR.next()
                gb[g] = (bN1, bN2)
                for bk in (bN1, bN2):
                    mm(psum[bk][:, :], zeros128, qT[:, 0:512], True, False, [b_const, b_q], [ps_buf[bk]], skip=True)
                emit_s(g * 4)
                for j in range(4):
                    c = g * 4 + j
                    if j < 3:
                        emit_s(c + 1)
                    jsl = slice(j * 128, (j + 1) * 128)
                    mi = c % 3
                    mm(psum[bN1][:, jsl], Vp[:, c, 0:128], mS[mi], False, False, [b_vp, b_mS[mi]], [ps_buf[bN1]], skip=True)
                    mm(psum[bN2][:, jsl], wrep[:, c, :], mS[mi], False, False, [b_wrep, b_mS[mi]], [ps_buf[bN2]], skip=True)

            def inter(g):
                bN1, bN2 = gb[g]
                for j in range(4):
                    c = g * 4 + j
                    if c == 0:
                        continue
                    csl = slice(c * 128, (c + 1) * 128)
                    jsl = slice(j * 128, (j + 1) * 128)
                    mm(psum[bN1][:, jsl], Cb_all[:, c, :], qT[:, csl], False, False, [b_Cb, b_q], [ps_buf[bN1]], skip=True)
                    mm(psum[bN2][:, jsl], nrep_all[:, c, :], qT[:, csl], False, False, [b_ktok, b_q], [ps_buf[bN2]], skip=True)

            def E1(g):
                bN1, bN2 = gb[g]
                r = g % 2
                P.dve(lambda e: e.scalar_tensor_tensor(out=d1[r], in0=psum[bN2][:, :], scalar=0.5, in1=ebrep[:, gsl(g)], op0=ALU.mult, op1=ALU.mult),
                      reads=[ps_buf[bN2], b_eb], writes=[b_d1[r]])
                P.dve(lambda e: e.scalar_tensor_tensor(out=d1[r], in0=d1[r], scalar=-1.0, in1=d1[r], op0=ALU.mult, op1=ALU.max),
                      reads=[b_d1[r]], writes=[b_d1[r]])
                P.dve(lambda e: e.tensor_scalar(out=d1[r], in0=d1[r], scalar1=1.0, scalar2=None, op0=ALU.max), reads=[b_d1[r]], writes=[b_d1[r]])
                P.act(lambda e: e.activation(out=d1[r], in_=d1[r], func=AF.Ln), reads=[b_d1[r]], writes=[b_d1[r]])
                P.act(lambda e: e.activation(out=d1[r], in_=d1[r], func=AF.Exp, scale=-1.0), reads=[b_d1[r]], writes=[b_d1[r]])

            def E2(g):
                bN1, bN2 = gb[g]
                r = g % 2
                P.dve(lambda e: e.scalar_tensor_tensor(out=d1[r], in0=d1[r], scalar=0.5, in1=ebrep[:, gsl(g)], op0=ALU.mult, op1=ALU.mult),
                      reads=[b_d1[r], b_eb], writes=[b_d1[r]])
                P.dve(lambda e: e.tensor_tensor(out=hg[r], in0=psum[bN1][:, :], in1=d1[r], op=ALU.mult),
                      reads=[ps_buf[bN1], b_d1[r]], writes=[b_hg[r]])
                P.act(lambda e: e.activation(out=sqn[r], in_=hg[r], func=AF.Square), reads=[b_hg[r]], writes=[b_sqn[r]])
                bst = RR.next()
                mm(psum[bst][:, :], on128, sqn[r], True, True, [b_const, b_sqn[r]], [ps_buf[bst]])
                rstd_from(bst, rsn[r], b_rsn[r], half=True)

            def E3(g):
                r = g % 2
                P.dve(lambda e: e.scalar_tensor_tensor(out=hg[r], in0=hg[r], scalar=ppc(PP_MON + h), in1=rsn[r], op0=ALU.mult, op1=ALU.mult),
                      reads=[b_hg[r], b_rsn[r], b_pp], writes=[b_hg[r]])
                P.dve(lambda e: e.scalar_tensor_tensor(out=yT[:, 8 + h, gsl(g)], in0=so[:, gsl(g)], scalar=1.0, in1=hg[r], op0=ALU.add, op1=ALU.mult),
                      reads=[b_hg[r], b_so], writes=[ybuf[8 + h][g]])

            intra(0)
            intra(1)
            P.dve(lambda e: e.tensor_tensor_scan(out=pre[:, 0:2064], data0=A3, data1=pre[:, 0:2064],
                                                 initial=0.0, op0=ALU.mult, op1=ALU.add), reads=[b_pre] + b_A3, writes=[b_pre])
            P.act(lambda e: e.activation(out=Cb_all[:, 1:16, :], in_=tU3[:, 0:128, 0:15].rearrange("p e c -> p c e"), func=AF.Identity),
                  reads=[b_pre], writes=[b_Cb])
            P.dve(lambda e: e.tensor_copy(out=nrep_all[:, 1:16, :], in_=tU3[:, 128, 0:15].unsqueeze(2).to_broadcast([128, 15, 128])),
                  reads=[b_pre], writes=[b_ktok])
            for step in (lambda: inter(0), lambda: E1(0), lambda: inter(1), lambda: E1(1), lambda: E2(0), lambda: intra(2), lambda: E3(0),
                         lambda: E2(1), lambda: inter(2), lambda: E1(2), lambda: E3(1), lambda: intra(3), lambda: E2(2), lambda: inter(3),
                         lambda: E1(3), lambda: E3(2), lambda: E2(3), lambda: E3(3)):
                step()

        for h in range(4):
            head(h)
        P.barrier()

    def qk_norm_fm(bankA, sq_t, b_sq, rs_t, b_rs, sel):
        P.act(lambda e: e.activation(out=sq_t, in_=psum[bankA][:, :], func=AF.Square), reads=[ps_buf[bankA]], writes=[b_sq])
        bst = RR.next()
        mm(psum[bst][:, :], sel, sq_t, True, True, [b_const, b_sq], [ps_buf[bst]])
        rstd_from(bst, rs_t, b_rs)

    def swa_phase(l):
        ppc = lambda c: pp[:, l * NPP + c:l * NPP + c + 1]

        BC = BufCache()

        def group(g):
            BC.rewind()
            S.reset()
            XS = Scratch(yT[:, 0:4, :].rearrange("p a t -> p (a t)"))
            qTs = XS.take([2, T], BF16)
            kTs0 = XS.take([T], BF16)
            Va = XS.take([NB, 128], BF16)
            kTm = [kTs0, S.take([T], BF16)]
            b_qs = BC.bufs(2, NG)
            b_ks = BC.bufs(NG)
            b_va = BC.buf()
            rope_sb = S.take([2, T], F32)
            b_rope = BC.buf()
            P.dma(lambda e: e.dma_start(out=rope_sb[:, 0, :], in_=rope_d[0]), writes=[b_rope])
            P.dma(lambda e: e.dma_start(out=rope_sb[:, 1, :], in_=rope_d[1]), writes=[b_rope])
            sq_t = [S.take([512], BF16) for _ in range(2)]
            rs_t = [S.take([512], F32)] * 2
            qn = [S.take([512], BF16) for _ in range(3)]
            t1 = S.take([512], F32)
            t2 = S.take([512], F32)
            b_sq, b_rs, b_qn = BC.bufs(2), [BC.buf()] * 2, BC.bufs(3)
            b_t1, b_t2 = BC.buf(), BC.buf()
            NPT = 4
            Pt = [S.take([256], BF16) for _ in range(NPT)]
            b_Pt = BC.bufs(NPT)
            rc = S.take([512], F32)
            b_rc = BC.buf()
            tq0, wq0 = load_w("w_in", l, O_SQ + g * 256, 128)
            tq1, wq1 = load_w("w_in", l, O_SQ + g * 256 + 128, 128)
            tk, wk = load_w("w_aux", l, g * 128, 128)
            tv, wvv = load_w("w_in", l, O_SV + g * 64, 64)
            P.pool(lambda e: e.memset(Va[:, :, 64:128], 1.0), writes=[b_va])
            P.pool(lambda e: e.memset(kTm[0][64:128, :], 0.0), writes=b_ks)
            P.pool(lambda e: e.memset(kTm[1][0:64, :], 0.0), writes=b_ks)
            for gg in range(4):
                bank = RR.next()
                for b in range(4):
                    blk = gg * 4 + b
                    for kc in range(KC):
                        mm(psum[bank][:, b * 64:(b + 1) * 64], hT[:, kc, blk * 128:(blk + 1) * 128], wvv[:, kc, :], kc == 0, kc == KC - 1,
                           [wt_buf[tv], hbuf[gg]], [ps_buf[bank]])
                P.act(lambda e, gg=gg, bank=bank: e.activation(out=Va[:, gg * 4:(gg + 1) * 4, 0:64], in_=psum[bank][:, 0:256].rearrange("p (a b) -> p a b", a=4),
                                                               func=AF.Identity), reads=[ps_buf[bank]], writes=[b_va])
            items = [(n, w) for n in range(NG) for w in range(3)]
            st = {}

            def stA(k):
                n, w = items[k]
                bank = AR.next()
                st[("A", k)] = bank
                if w < 2:
                    proj_fm((wq0, wq1)[w], (tq0, tq1)[w], n, bank)
                else:
                    proj_fm(wk, tk, n, bank)

            def stB(k):
                bank = st[("A", k)]
                r = k % 2
                P.act(lambda e, r=r, bank=bank: e.activation(out=sq_t[r], in_=psum[bank][:, :], func=AF.Square), reads=[ps_buf[bank]], writes=[b_sq[r]])
                bst = RR.next()
                st[("B", k)] = bst
                mm(psum[bst][:, :], bd64, sq_t[r], True, True, [b_const, b_sq[r]], [ps_buf[bst]])

            def stC(k):
                bank, bst = st[("A", k)], st[("B", k)]
                r = k % 2
                q3 = k % 3
                rstd_from(bst, rs_t[r], b_rs[r])
                P.dve(lambda e, r=r, q3=q3, bank=bank: e.tensor_tensor(out=qn[q3], in0=psum[bank][:, :], in1=rs_t[r], op=ALU.mult),
                      reads=[ps_buf[bank], b_rs[r]], writes=[b_qn[q3]])
                bC = RR.next()
                st[("C", k)] = bC
                mm(psum[bC][:, :], rperm, qn[q3], True, True, [b_const, b_qn[q3]], [ps_buf[bC]])

            def stD(k):
                n, w = items[k]
                bC = st[("C", k)]
                q3 = k % 3
                gc, gs = (PP_SQN, PP_SQNS) if w < 2 else (PP_SKN, PP_SKNS)
                if w < 2:
                    dsts, bd = [(slice(0, 128), qTs[:, w, gsl(n)])], b_qs[w][n]
                else:
                    dsts, bd = [(slice(0, 64), kTm[0][0:64, gsl(n)]), (slice(64, 128), kTm[1][64:128, gsl(n)])], b_ks[n]
                P.dve(lambda e, q3=q3, gc=gc, n=n: e.scalar_tensor_tensor(out=t1, in0=qn[q3], scalar=ppc(gc), in1=rope_sb[:, 0, gsl(n)], op0=ALU.mult, op1=ALU.mult),
                      reads=[b_qn[q3], b_rope, b_pp], writes=[b_t1])
                P.dve(lambda e, gs=gs, n=n, bC=bC: e.scalar_tensor_tensor(out=t2, in0=psum[bC][:, :], scalar=ppc(gs), in1=rope_sb[:, 1, gsl(n)], op0=ALU.mult, op1=ALU.mult),
                      reads=[ps_buf[bC], b_rope, b_pp], writes=[b_t2])
                for rws, dst in dsts:
                    P.pool(lambda e, dst=dst, rws=rws: e.tensor_tensor(out=dst, in0=t1[rws, :], in1=t2[rws, :], op=ALU.add), reads=[b_t1, b_t2], writes=[bd])

            stages = (stA, stB, stC, stD)
            for step in range(len(items) + len(stages) - 1):
                for si, f in enumerate(stages):
                    k = step - si
                    if 0 <= k < len(items):
                        f(k)
            its = []
            for qc in range(2):
                for hh in range(2):
                    for I in range(NG):
                        jl = list(range(max(4 * I - 1, 0), 4 * I + 4))
                        for j in jl:
                            its.append((qc, hh, I, j, j == jl[0], j == jl[-1]))
            LA = 2

            def emit_s(k):
                qc, hh, I, j, first, last = its[k]
                rows = slice(hh * 64, hh * 64 + 64)
                i_lo, i_hi = max(j, 4 * I), min(j + 1, 4 * I + 3)
                ncol = (i_hi - i_lo + 1) * 128
                bS = RR.next()
                mm(psum[bS][:, 0:ncol], kTm[hh][:, j * 128:(j + 1) * 128], qTs[:, qc, i_lo * 128:(i_hi + 1) * 128], True, False,
                   [b_ks[j // 4], b_qs[qc][I]], [ps_buf[bS]])
                msk = negUL[:, 0:ncol] if i_lo == j else negL
                mm(psum[bS][:, 0:ncol], ident, msk, False, True, [b_const], [ps_buf[bS]])
                p = k % NPT
                P.act(lambda e, p=p, bS=bS, ncol=ncol: e.activation(out=Pt[p][:, 0:ncol], in_=psum[bS][:, 0:ncol], func=AF.Exp, scale=0.125),
                      reads=[ps_buf[bS]], writes=[b_Pt[p]])

            def emit_pv(k):
                qc, hh, I, j, first, last = its[k]
                rows = slice(hh * 64, hh * 64 + 64)
                head = g * 4 + qc * 2 + hh
                ych = 4 + g * 2 + qc
                i_lo, i_hi = max(j, 4 * I), min(j + 1, 4 * I + 3)
                ncol = (i_hi - i_lo + 1) * 128
                col0 = (i_lo - 4 * I) * 128
                if first:
                    st["bO"] = AR.next()
                bO = st["bO"]
                p = k % NPT
                mm(psum[bO][:, col0:col0 + ncol], Va[:, j, :], Pt[p][:, 0:ncol], first, last, [b_va, b_Pt[p]], [ps_buf[bO]], skip=True)
                if last:
                    P.act(lambda e, bO=bO, head=head: e.activation(out=rc[0:64, :], in_=psum[bO][64:128, :], func=AF.Ln, bias=esink[64:128, head:head + 1]),
                          reads=[ps_buf[bO], b_scr], writes=[b_rc])
                    P.act(lambda e: e.activation(out=rc[0:64, :], in_=rc[0:64, :], func=AF.Exp, scale=-1.0), reads=[b_rc], writes=[b_rc])
                    P.dve(lambda e, bO=bO, rows=rows, ych=ych, I=I: e.tensor_tensor(out=yT[rows, ych, gsl(I)], in0=psum[bO][0:64, :], in1=rc[0:64, :], op=ALU.mult),
                          reads=[ps_buf[bO], b_rc], writes=[ybuf[ych][I]])

            for k in range(len(its) + LA):
                if k < len(its):
                    emit_s(k)
                if k >= LA:
                    emit_pv(k - LA)

        for g in range(2):
            group(g)
        P.barrier()

    def fox_phase(l, csp, offsp):
        ppc = lambda c: pp[:, l * NPP + c:l * NPP + c + 1]
        v3 = lambda a: a.rearrange("p (b c) -> p b c", b=16)
        v4 = lambda a: a.rearrange("p (I i c) -> p I i c", I=4, i=4)

        BC = BufCache()

        def pair(c):
            BC.rewind()
            S.reset()
            qTm = [S.take([T], BF16) for _ in range(2)]
            kTm = [S.take([T], BF16) for _ in range(2)]
            Va = S.take([NB, 2, 128], BF16)
            bias = S.take([2, 16, 4], F32)
            dsh = S.take([2, 16], F32)
            rsh = S.take([2, 16], F32)
            hi16 = S.take([2, 16], BF16)
            v1 = S.take([2, 16], F32)
            valb = S.take([2, 16], BF16)
            b_q, b_k = BC.bufs(NG), BC.bufs(NG)
            b_va, b_bias, b_sh = BC.buf(), BC.buf(), BC.buf()
            sq_t = [S.take([512], BF16) for _ in range(2)]
            rs_t = [S.take([512], F32)] * 2
            b_sq, b_rs = BC.bufs(2), [BC.buf()] * 2
            NPT = 4
            Pt = [S.take([512], BF16) for _ in range(NPT)]
            b_Pt = BC.bufs(NPT)
            rc = [S.take([512], F32)] * 2
            b_rc = [BC.buf()] * 2
            tq, wq = load_w("w_in", l, O_FQ + c * 128, 128)
            tk, wk = load_w("w_in", l, O_FK + c * 128, 128)
            tv, wvv = load_w("w_in", l, O_FV + c * 128, 128)
            for hh in range(2):
                h = 2 * c + hh
                P.dve(lambda e, hh=hh, h=h: e.tensor_tensor(out=bias[:, hh, :, :], in0=v3(csp)[:, :, h:h + 1].to_broadcast([128, 16, 4]),
                                                            in1=v4(offsp)[:, :, 0, h].unsqueeze(1).to_broadcast([128, 16, 4]), op=ALU.subtract),
                      reads=[b_scr], writes=[b_bias])
                d4 = lambda a: a.rearrange("p (I i) -> p I i", I=4)
                P.dve(lambda e, hh=hh, h=h: e.tensor_tensor(out=d4(dsh[:, hh, :]), in0=v4(offsp)[:, :, :, h],
                                                            in1=v4(offsp)[:, :, 0:1, h].to_broadcast([128, 4, 4]), op=ALU.subtract),
                      reads=[b_scr], writes=[b_sh])
                P.dve(lambda e, hh=hh: e.tensor_scalar(out=hi16[:, hh, :], in0=dsh[:, hh, :], scalar1=-8.0, scalar2=None, op0=ALU.mult),
                      reads=[b_sh], writes=[b_sh])
                P.dve(lambda e, hh=hh: e.scalar_tensor_tensor(out=rsh[:, hh, :], in0=dsh[:, hh, :], scalar=-8.0, in1=hi16[:, hh, :],
                                                              op0=ALU.mult, op1=ALU.subtract), reads=[b_sh], writes=[b_sh])
                P.dve(lambda e, hh=hh: e.tensor_scalar(out=v1[:, hh, :], in0=hi16[:, hh, :], scalar1=cf32[:, 256:257], scalar2=None, op0=ALU.mult),
                      reads=[b_sh, b_const], writes=[b_sh])
                P.dve(lambda e, hh=hh: e.scalar_tensor_tensor(out=valb[:, hh, :], in0=rsh[:, hh, :], scalar=cf32[:, 257:258], in1=v1[:, hh, :],
                                                              op0=ALU.mult, op1=ALU.add), reads=[b_sh, b_const], writes=[b_sh])
                oth = slice(64, 128) if hh == 0 else slice(0, 64)
                P.dve(lambda e, hh=hh, oth=oth: e.tensor_copy(out=qTm[hh][oth, :].rearrange("p (a b) -> p a b", a=16),
                                                               in_=valb[oth, hh, :].unsqueeze(2).to_broadcast([64, 16, 128])),
                      reads=[b_sh], writes=b_q)
                if c == 0:
                    P.act(lambda e, hh=hh, oth=oth: e.activation(out=kTm[hh][oth, :], in_=cf32[oth, 258:259].to_broadcast([64, T]), func=AF.Identity),
                          reads=[b_const], writes=b_k)
            P.pool(lambda e: e.memset(Va[:, :, :, 64:128], 1.0), writes=[b_va])
            for gg in range(4):
                bank = RR.next()
                for b in range(4):
                    blk = gg * 4 + b
                    for kc in range(KC):
                        mm(psum[bank][:, b * 128:(b + 1) * 128], hT[:, kc, blk * 128:(blk + 1) * 128], wvv[:, kc, :], kc == 0, kc == KC - 1,
                           [wt_buf[tv], hbuf[gg]], [ps_buf[bank]])
                P.dve(lambda e, gg=gg, bank=bank: e.tensor_copy(out=Va[:, gg * 4:(gg + 1) * 4, :, 0:64],
                                                                in_=psum[bank][:, :].rearrange("p (a b c) -> p a b c", a=4, b=2)),
                      reads=[ps_buf[bank]], writes=[b_va])
            kcnt = [0]

            def setup_n(n):
                for which in range(2):
                    r = kcnt[0] % 2
                    kcnt[0] += 1
                    bank = RR.next()
                    if which == 0:
                        proj_fm(wq, tq, n, bank)
                    else:
                        proj_fm(wk, tk, n, bank)
                    qk_norm_fm(bank, sq_t[r], b_sq[r], rs_t[r], b_rs[r], bd64)
                    dstm, bdd, gcol = (qTm, b_q, PP_FQN) if which == 0 else (kTm, b_k, PP_FKN)
                    for hh in range(2):
                        rows = slice(hh * 64, hh * 64 + 64)
                        P.dve(lambda e, r=r, bank=bank, n=n, hh=hh, rows=rows, dstm=dstm, gcol=gcol: e.scalar_tensor_tensor(
                            out=dstm[hh][rows, gsl(n)], in0=psum[bank][rows, :], scalar=pp[rows, l * NPP + gcol:l * NPP + gcol + 1], in1=rs_t[r][rows, :],
                            op0=ALU.mult, op1=ALU.mult), reads=[ps_buf[bank], b_rs[r], b_pp], writes=[bdd[n]])

            setup_n(0)
            setup_n(1)
            its = [(hh, I, j) for hh in range(2) for I in range(NG) for j in range(4 * I + 4)]
            LA = 2
            st = {}

            def emit_s(k):
                hh, I, j = its[k]
                t0 = max(j, 4 * I)
                col0 = (t0 - 4 * I) * 128
                diag = j >= 4 * I
                bS = RR.next()
                mm(psum[bS][:, col0:512], kTm[hh][:, j * 128:(j + 1) * 128], qTm[hh][:, I * 512 + col0:(I + 1) * 512], True, not diag,
                   [b_k[j // 4], b_q[I]], [ps_buf[bS]])
                if diag:
                    mm(psum[bS][:, col0:col0 + 128], ident, negU, False, True, [b_const], [ps_buf[bS]])
                p = k % NPT
                P.act(lambda e, p=p, bS=bS, col0=col0, hh=hh, j=j, I=I: e.activation(out=Pt[p][:, col0:512], in_=psum[bS][:, col0:512], func=AF.Exp, scale=0.125,
                                                                                    bias=bias[:, hh, j, I:I + 1]),
                      reads=[ps_buf[bS], b_bias], writes=[b_Pt[p]])

            def emit_pv(k):
                hh, I, j = its[k]
                nj = 4 * I + 4
                if j == 0:
                    st["bO"] = AR.next()
                bO = st["bO"]
                t0 = max(j, 4 * I)
                col0 = (t0 - 4 * I) * 128
                p = k % NPT
                mm(psum[bO][:, col0:512], Va[:, j, hh, :], Pt[p][:, col0:512], j == 0, j == nj - 1, [b_va, b_Pt[p]], [ps_buf[bO]])
                if j == nj - 1:
                    rows = slice(hh * 64, hh * 64 + 64)
                    r = I % 2
                    P.dve(lambda e, r=r, bO=bO: e.reciprocal(out=rc[r][0:64, :], in_=psum[bO][64:128, :]), reads=[ps_buf[bO]], writes=[b_rc[r]])
                    P.dve(lambda e, r=r, bO=bO, rows=rows, I=I: e.tensor_tensor(out=yT[rows, c, gsl(I)], in0=psum[bO][0:64, :], in1=rc[r][0:64, :], op=ALU.mult),
                          reads=[ps_buf[bO], b_rc[r]], writes=[ybuf[c][I]])

            for k in range(len(its) + LA):
                if k < len(its):
                    if its[k] == (0, 0, 0):
                        setup_n(2)
                    elif its[k] == (0, 1, 0):
                        setup_n(3)
                    emit_s(k)
                if k >= LA:
                    emit_pv(k - LA)

        for c in range(4):
            pair(c)
        P.barrier()

    def merge_phase(l):
        S.reset()
        mg = [S.take([KC, 512], BF16) for _ in range(2)]
        b_mg = bufs(2)
        sg = [S.take([512], F32) for _ in range(3)]
        b_sg = bufs(3)
        accm = [S.take([512], F32) for _ in range(2)]
        tmpm = [S.take([512], F32) for _ in range(2)]
        b_accm, b_tmpm = bufs(2), bufs(2)
        si = 0
        k = 0
        for n in range(NG):
            mgn, b_mgn = mg[n % 2], b_mg[n % 2]
            for m in range(KC):
                r = k % 2
                k += 1
                for b in range(3):
                    tg, wg = load_w("w_in", l, O_GL + b * D + m * 128, 128)
                    tb, wb = load_w("w_br", l, m * 128, 128, kc0=4 * b, nkc=4)
                    bG = RR.next()
                    proj_fm(wg, tg, n, bG)
                    s_ = si % 3
                    si += 1
                    P.act(lambda e, s_=s_, bG=bG: e.activation(out=sg[s_], in_=psum[bG][:, :], func=AF.Sigmoid), reads=[ps_buf[bG]], writes=[b_sg[s_]])
                    bP = RR.next()
                    for kc in range(4):
                        mm(psum[bP][:, :], wb[:, kc, :], yT[:, 4 * b + kc, gsl(n)], kc == 0, kc == 3,
                           [wt_buf[tb], ybuf[4 * b + kc][n]], [ps_buf[bP]])
                    if b == 0:
                        P.dve(lambda e, s_=s_, bP=bP, r=r: e.tensor_tensor(out=accm[r], in0=psum[bP][:, :], in1=sg[s_], op=ALU.mult),
                              reads=[ps_buf[bP], b_sg[s_]], writes=[b_accm[r]])
                    else:
                        P.dve(lambda e, s_=s_, bP=bP, r=r: e.tensor_tensor(out=tmpm[r], in0=psum[bP][:, :], in1=sg[s_], op=ALU.mult),
                              reads=[ps_buf[bP], b_sg[s_]], writes=[b_tmpm[r]])
                        if b == 1:
                            P.pool(lambda e, r=r: e.tensor_tensor(out=accm[r], in0=accm[r], in1=tmpm[r], op=ALU.add),
                                   reads=[b_accm[r], b_tmpm[r]], writes=[b_accm[r]])
                        else:
                            P.pool(lambda e, r=r, m=m, mgn=mgn: e.tensor_tensor(out=mgn[:, m, :], in0=accm[r], in1=tmpm[r], op=ALU.add),
                                   reads=[b_accm[r], b_tmpm[r]], writes=[b_mgn])
            for m in range(KC):
                to, wo = load_w("w_out", l, m * 128, 128)
                bank = AR.next()
                for kc in range(KC):
                    mm(psum[bank][:, :], wo[:, kc, :], mgn[:, kc, :], kc == 0, kc == KC - 1, [wt_buf[to], b_mgn], [ps_buf[bank]])
                P.dve(lambda e, m=m, n=n, bank=bank: e.tensor_tensor(out=xT[:, m, gsl(n)], in0=psum[bank][:, :], in1=xT[:, m, gsl(n)], op=ALU.add),
                      reads=[ps_buf[bank], xbuf[n]], writes=[xbuf[n]])
        P.barrier()

    def mlp_phase(l, after_n=None):
        S.reset()
        XS = Scratch(yT[:, :, :].rearrange("p a t -> p (a t)"))
        actT = XS.take([32, 512], BF16)
        b_act = bufs(32)
        rl = [S.take([512], F32) for _ in range(3)]
        b_rl = bufs(3)
        ri = 0
        for n in range(NG):
            for f in range(32):
                tu, wu = load_w("w_up", l, f * 128, 128)
                bank = RR.next()
                proj_fm(wu, tu, n, bank)
                r = ri % 3
                ri += 1
                P.act(lambda e, r=r, bank=bank: e.activation(out=rl[r], in_=psum[bank][:, :], func=AF.Relu), reads=[ps_buf[bank]], writes=[b_rl[r]])
                P.pool(lambda e, r=r, f=f: e.tensor_tensor(out=actT[:, f, :], in0=rl[r], in1=rl[r], op=ALU.mult), reads=[b_rl[r]], writes=[b_act[f]])
            for m in range(KC):
                bank = AR.next()
                for qd in range(4):
                    td, wd = load_w("w_dn", l, m * 128, 128, kc0=qd * 8, nkc=8)
                    for kc in range(8):
                        kk = qd * 8 + kc
                        mm(psum[bank][:, :], wd[:, kc, :], actT[:, kk, :], kk == 0, kk == 31, [wt_buf[td], b_act[kk]], [ps_buf[bank]])
                P.dve(lambda e, m=m, n=n, bank=bank: e.tensor_tensor(out=xT[:, m, gsl(n)], in0=psum[bank][:, :], in1=xT[:, m, gsl(n)], op=ALU.add),
                      reads=[ps_buf[bank], xbuf[n]], writes=[xbuf[n]])
            if after_n is not None and n > 0:
                after_n(n - 1)
        if after_n is not None:
            after_n(NG - 1)
        P.barrier()

    final = []

    def load_x(s, n):
        P.dma(lambda e: e.dma_start(out=xT[:, :, gsl(n)], in_=xT_d[s, :, gsl(n)].rearrange("(kc p) t -> p kc t", p=128)), writes=[xbuf[n]])

    def store_x(s, n):
        final.append(P.dma(lambda e: e.dma_start(out=yT_d[s, :, gsl(n)].rearrange("(kc p) t -> p kc t", p=128), in_=xT[:, :, gsl(n)]), reads=[xbuf[n]]))

    for s in range(n_seq):
        if s == 0:
            for n in range(NG):
                load_x(0, n)
        for l in range(depth):
            rmsnorm_x(l, PP_GMIX)
            csp, offsp = gates_phase(l)
            if "M" in phases:
                mlstm_phase(l)
            if "S" in phases:
                swa_phase(l)
            if "F" in phases:
                fox_phase(l, csp, offsp)
            if dbg and l == 0 and s == 0:
                for c in range(12):
                    final.append(P.dma(lambda e, c=c: e.dma_start(out=dbg_y[c], in_=yT[:, c, :]), reads=ybuf[c]))
            if "G" in phases:
                merge_phase(l)
            if dbg and l == 0 and s == 0:
                for kc in range(KC):
                    final.append(P.dma(lambda e, kc=kc: e.dma_start(out=dbg_x1[kc], in_=xT[:, kc, :]), reads=xbuf))
            last = (l == depth - 1)

            def after_n(n, s=s, last=last):
                if last:
                    store_x(s, n)
                    if s + 1 < n_seq:
                        load_x(s + 1, n)

            if "P" in phases:
                rmsnorm_x(l, PP_GMLP)
                mlp_phase(l, after_n)
            else:
                for n in range(NG):
                    after_n(n)
            if l == depth - 1 or s > 0:
                flush_casts()
    P.emit(final_ops=final)
    return nc


def _consts():
    bf = ml_dtypes.bfloat16
    s = np.arange(128)[:, None]
    t = np.arange(128)[None, :]
    ident = (s == t).astype(np.float32)
    mU = (s <= t).astype(np.float32)
    mL = (s > t).astype(np.float32)
    bd = ((s // 64) == (t // 64)).astype(np.float32) / 64.0
    on128 = np.full((128, 128), 1.0 / 128.0, np.float32)
    rperm = ((s // 64 == t // 64) & ((s % 64) == ((t % 64) + 32) % 64)).astype(np.float32)
    on1024 = np.full((128, 128), 1.0 / 1024.0, np.float32)
    sel2 = np.zeros((128, 128), np.float32)
    sel2[0, :] = 1.0
    sel2[32, :] = 1.0
    NEG = -30000.0
    negU = np.where(s <= t, 0.0, NEG).astype(np.float32)
    negL = np.where(s > t, 0.0, NEG).astype(np.float32)
    pad = np.zeros((128, 128), np.float32)
    cmat = np.concatenate([ident, mU, mL, bd, on128, rperm, on1024, sel2, negU, negL, pad], axis=1).astype(bf)
    pcol = np.arange(128)
    selc = np.stack([(pcol % 64 == 0), (pcol % 64 == 32), (pcol % 32 == 0), np.zeros(128, bool)], axis=1).astype(np.float32)
    cf32 = np.concatenate([mU, np.ones((128, 128), np.float32), selc], axis=1).astype(np.float32)
    inv = (10000.0 ** (-np.arange(0, 64, 2, dtype=np.float32) / np.float32(64))).astype(np.float32)
    ang = np.arange(T, dtype=np.float32)[:, None] * inv[None, :]
    cos, sin = np.cos(ang).astype(np.float32), np.sin(ang).astype(np.float32)
    r = np.arange(128)
    cosT = cos[:, r % 32].T.copy()
    sgn = np.where((r % 64) < 32, -1.0, 1.0).astype(np.float32)
    sinT = (sin[:, r % 32].T * sgn[:, None]).astype(np.float32)
    rope = np.stack([cosT, sinT]).astype(np.float32)
    return cmat, cf32, rope


def _prep_params(inp, depth):
    pp = np.zeros((128, depth * NPP), np.float32)
    brow = np.zeros((128, depth * 16), np.float32)
    p = np.arange(128)
    for l in range(depth):
        o = l * NPP
        pp[:, o + PP_GMIX:o + PP_GMIX + 8] = inp["norm_mix"][l].reshape(8, 128).T
        pp[:, o + PP_GMLP:o + PP_GMLP + 8] = inp["norm_mlp"][l].reshape(8, 128).T
        pp[:, o + PP_FQN] = inp["fox_q_norm"][l][p % 64]
        pp[:, o + PP_FKN] = inp["fox_k_norm"][l][p % 64]
        pp[:, o + PP_SQN] = inp["swa_q_norm"][l][p % 64]
        pp[:, o + PP_SKN] = inp["swa_k_norm"][l][p % 64]
        pp[:, o + PP_SQNS] = inp["swa_q_norm"][l][(p % 64 + 32) % 64]
        pp[:, o + PP_SKNS] = inp["swa_k_norm"][l][(p % 64 + 32) % 64]
        cw = inp["conv_w"][l]
        for cc in range(8):
            for j in range(4):
                pp[:, o + PP_CONVW + cc * 4 + j] = cw[j, cc * 128:(cc + 1) * 128]
            pp[:, o + PP_CONVB + cc] = inp["conv_b"][l][cc * 128:(cc + 1) * 128]
        pp[:, o + PP_MON:o + PP_MON + 4] = inp["mlstm_out_norm"][l].reshape(4, 128).T
        pp[:, o + PP_MFB:o + PP_MFB + 4] = inp["mlstm_f_bias"][l][None, :]
        pp[:, o + PP_SINK:o + PP_SINK + 8] = inp["swa_sinks"][l][None, :]
        brow[:, l * 16:l * 16 + 8] = inp["fox_f_bias"][l][None, :]
        brow[:, l * 16 + 8:l * 16 + 12] = inp["mlstm_i_bias"][l][None, :]
        brow[:, l * 16 + 12:l * 16 + 16] = inp["mlstm_f_bias"][l][None, :]
    return pp, brow


def _prep_waux(w_in):
    depth = w_in.shape[0]
    aux = np.empty((depth, D, NAUX), np.float32)
    for g in range(2):
        k = w_in[:, :, O_SK + g * 64:O_SK + (g + 1) * 64]
        aux[:, :, g * 128:g * 128 + 64] = k
        aux[:, :, g * 128 + 64:g * 128 + 128] = k
    for h in range(4):
        aux[:, :, 256 + h * 128:256 + (h + 1) * 128] = w_in[:, :, O_MF + h:O_MF + h + 1]
    aux[:, :, 768:776] = w_in[:, :, O_FF:O_FF + 8]
    aux[:, :, 776:780] = w_in[:, :, O_MI:O_MI + 4]
    aux[:, :, 780:784] = w_in[:, :, O_MF:O_MF + 4]
    return aux


_NC_CACHE = {}


def _get_nc(n_seq, depth, dbg=False, phases="MSFGP"):
    key = (n_seq, depth, dbg, phases)
    if key not in _NC_CACHE:
        _NC_CACHE[key] = build(n_seq, depth, dbg, phases)
    return _NC_CACHE[key]


def make_maps(inp, n_cores, n_seq, depth):
    f = lambda a: np.ascontiguousarray(np.asarray(a, dtype=np.float32))
    cmat, cf32, rope = _consts()
    pp, brow = _prep_params(inp, depth)
    w_in = f(inp["w_in"][:depth])
    shared = {
        "w_in": w_in, "w_aux": _prep_waux(w_in), "w_br": f(inp["w_branch"][:depth]).reshape(depth, 1536, D),
        "w_out": f(inp["w_out"][:depth]), "w_up": f(inp["w_up"][:depth]), "w_dn": f(inp["w_down"][:depth]),
        "pp": pp, "brow": brow, "rope": rope, "cmat": cmat, "cf32": cf32,
    }
    x = np.asarray(inp["x"], dtype=np.float32)
    maps = []
    for c in range(n_cores):
        xs = x[c * n_seq:(c + 1) * n_seq]
        m = dict(shared)
        m["xT"] = np.ascontiguousarray(xs.transpose(0, 2, 1))
        maps.append(m)
    return maps


def kernel(**inputs):
    n_seq = 32 // NCORES
    nc = _get_nc(n_seq, DEPTH)
    maps = make_maps(inputs, NCORES, n_seq, DEPTH)
    res = run_bass_kernel_spmd(nc, maps, core_ids=list(range(NCORES)))
    outs = [np.asarray(r["yT"]).transpose(0, 2, 1) for r in res.results]
    return np.ascontiguousarray(np.concatenate(outs, axis=0).astype(np.float32))
```
